# Optimizing a Trainium2 kernel written in Bass

```python
import math
import jax, jax.numpy as jnp
from jax import lax
import numpy as np

D_MODEL = 2048
BATCH = 4
SEQ = 2048
DEPTH = 1
DEC_BATCH = 128
DEC_SEQ = 8
PAST_LEN = 16384
PAGE_SIZE = 128

MIX_WIDTH = D_MODEL
S5_WIDTH = MIX_WIDTH // 2
S5_GROUP = 16
S5_GROUPS = S5_WIDTH // S5_GROUP
S5_STATE = 64
RET_WIDTH = MIX_WIDTH - S5_WIDTH
RET_HEADS = 8
RET_HEAD_DIM = RET_WIDTH // RET_HEADS
RET_CHUNK = 128
ROPE_BASE = 10000.0
D_FF = ((8 * D_MODEL + 2) // 3 + 255) // 256 * 256
IN_WIDTH = S5_WIDTH + 4 * RET_WIDTH
NORM_EPS = 1e-6

kernel_name = "hybrid_s5_retention_decode_step"


def rms_norm(x, g):
    xf = x.astype(jnp.float32)
    y = xf * lax.rsqrt(jnp.mean(xf * xf, axis=-1, keepdims=True) + NORM_EPS)
    return (y * g.astype(jnp.float32)).astype(x.dtype)


def rotary(x, pos):
    half = x.shape[-1] // 2
    inv_freq = ROPE_BASE ** (-jnp.arange(half, dtype=jnp.float32) / half)
    ang = pos[:, None] * inv_freq[None, :]
    cos, sin = jnp.cos(ang), jnp.sin(ang)
    x1, x2 = x[..., :half], x[..., half:]
    return jnp.concatenate([x1 * cos - x2 * sin, x1 * sin + x2 * cos], axis=-1)


def _lin_combine(e_i, e_j):
    a_i, b_i = e_i
    a_j, b_j = e_j
    return a_j * a_i, a_j * b_i + b_j


def s5_mixer(u, x0_re, x0_im, lam_re, lam_im, log_step, b_re, b_im, c_re, c_im, d, w_glu, b_glu):
    n, l, _ = u.shape
    f32 = jnp.float32
    lam = lax.complex(lam_re.astype(f32), lam_im.astype(f32))
    dt = jnp.exp(log_step.astype(f32))
    lam_dt = lam * dt[:, None]
    lam_bar = jnp.exp(lam_dt)
    b = lax.complex(b_re.astype(f32), b_im.astype(f32))
    b_bar = ((lam_bar - 1.0) / lam)[..., None] * b
    c = lax.complex(c_re.astype(f32), c_im.astype(f32))
    uf = u.astype(f32)
    ug = uf.reshape(n, l, S5_GROUPS, S5_GROUP)
    bu = jnp.einsum('nlgh,gph->lngp', ug.astype(jnp.complex64), b_bar)
    x0 = lax.complex(x0_re.astype(f32), x0_im.astype(f32))
    bu = bu.at[0].add(lam_bar[None] * x0)
    a = jnp.broadcast_to(lam_bar, (l, 1, S5_GROUPS, S5_STATE))
    _, xs = lax.associative_scan(_lin_combine, (a, bu), axis=0)
    y = jnp.einsum('lngp,ghp->nlgh', xs, c).real.reshape(n, l, S5_WIDTH)
    y = y + d.astype(f32) * uf
    y = jax.nn.gelu(y)
    y = y * jax.nn.sigmoid(y @ w_glu.astype(f32) + b_glu.astype(f32))
    x_last = xs[-1]
    return y.astype(u.dtype), jnp.real(x_last), jnp.imag(x_last)


def retention_chunked(q, k, v, r0):
    n, h, l, dk = q.shape
    dv = v.shape[-1]
    chunk = math.gcd(l, RET_CHUNK)
    nc = l // chunk
    log_gamma = jnp.log(1.0 - 2.0 ** (-5.0 - jnp.arange(h, dtype=jnp.float32)))
    idx = jnp.arange(chunk, dtype=jnp.float32)
    diff = idx[:, None] - idx[None, :]
    mask = jnp.where(diff >= 0, jnp.exp(log_gamma[:, None, None] * jnp.maximum(diff, 0.0)), 0.0)
    q_decay = jnp.exp(log_gamma[:, None] * (idx + 1.0))[..., None]
    k_decay = jnp.exp(log_gamma[:, None] * (chunk - 1.0 - idx))[..., None]
    chunk_decay = jnp.exp(log_gamma * chunk)[:, None, None]

    def to_chunks(t):
        return t.reshape(n, h, nc, chunk, t.shape[-1]).transpose(2, 0, 1, 3, 4)

    def step(r, inp):
        qc, kc, vc = inp
        scores = jnp.einsum('bhnd,bhmd->bhnm', qc, kc) * mask
        o = jnp.einsum('bhnm,bhme->bhne', scores, vc) + jnp.einsum('bhnd,bhde->bhne', qc, r) * q_decay
        r_new = r * chunk_decay + jnp.einsum('bhmd,bhme->bhde', kc * k_decay, vc)
        return r_new, o

    r_last, o = lax.scan(step, r0, (to_chunks(q), to_chunks(k), to_chunks(v)))
    o = o.transpose(1, 2, 0, 3, 4).reshape(n, h, l, dv)
    return o, r_last


def hybrid_layer(x, s5_re0, s5_im0, ret0, pos0,
                 norm_mix, w_in, lam_re, lam_im, log_step, b_re, b_im, c_re, c_im, d, w_glu, b_glu,
                 ret_gn_w, w_out, norm_ffn, w_ffn_in, w_ffn_out):
    n, l, _ = x.shape
    f32 = jnp.float32
    h = rms_norm(x, norm_mix)
    proj = h @ w_in
    u, q, k, v, g = jnp.split(proj, [S5_WIDTH, S5_WIDTH + RET_WIDTH, S5_WIDTH + 2 * RET_WIDTH,
                                     S5_WIDTH + 3 * RET_WIDTH], axis=-1)
    s5_out, s5_re, s5_im = s5_mixer(u, s5_re0, s5_im0, lam_re, lam_im, log_step,
                                    b_re, b_im, c_re, c_im, d, w_glu, b_glu)
    def heads(t):
        return t.astype(f32).reshape(n, l, RET_HEADS, RET_HEAD_DIM).transpose(0, 2, 1, 3)
    pos = pos0 + jnp.arange(l, dtype=f32)
    qh = rotary(heads(q), pos)
    kh = rotary(heads(k), pos) * (RET_HEAD_DIM ** -0.5)
    vh = heads(v)
    o, ret_new = retention_chunked(qh, kh, vh, ret0.astype(f32))
    mu = jnp.mean(o, axis=-1, keepdims=True)
    var = jnp.mean(jnp.square(o - mu), axis=-1, keepdims=True)
    o = (o - mu) * lax.rsqrt(var + NORM_EPS)
    o = o.transpose(0, 2, 1, 3).reshape(n, l, RET_WIDTH) * ret_gn_w.astype(f32)
    ret_out = (jax.nn.silu(g.astype(f32)) * o).astype(x.dtype)
    x = x + jnp.concatenate([s5_out, ret_out], axis=-1) @ w_out
    h2 = rms_norm(x, norm_ffn)
    gate, up = jnp.split(h2 @ w_ffn_in, 2, axis=-1)
    x = x + (jax.nn.silu(gate) * up) @ w_ffn_out
    return x, s5_re, s5_im, ret_new


def setup_inputs(seed: int = 0) -> dict:
    key = jax.random.key(seed)
    ks = jax.random.split(key, 24)
    nrm = jax.random.normal
    n_idx = jnp.arange(S5_STATE, dtype=jnp.float32)
    return {
        "x_prompt": nrm(ks[0], (BATCH, SEQ, D_MODEL), jnp.float32),
        "x_sample": nrm(ks[1], (DEC_BATCH, DEC_SEQ, D_MODEL), jnp.float32),
        "state_s5_re": 0.5 * nrm(ks[2], (DEPTH, DEC_BATCH, S5_GROUPS, S5_STATE), jnp.float32),
        "state_s5_im": 0.5 * nrm(ks[3], (DEPTH, DEC_BATCH, S5_GROUPS, S5_STATE), jnp.float32),
        "state_ret": 0.5 * nrm(ks[4], (DEPTH, DEC_BATCH, RET_HEADS, RET_HEAD_DIM, RET_HEAD_DIM), jnp.float32),
        "norm_mix": 1.0 + 0.02 * nrm(ks[5], (DEPTH, D_MODEL), jnp.float32),
        "w_in": nrm(ks[6], (DEPTH, D_MODEL, IN_WIDTH), jnp.float32) * D_MODEL ** -0.5,
        "s5_lambda_re": -0.5 + 0.01 * nrm(ks[7], (DEPTH, S5_GROUPS, S5_STATE), jnp.float32),
        "s5_lambda_im": math.pi * n_idx + 0.01 * nrm(ks[8], (DEPTH, S5_GROUPS, S5_STATE), jnp.float32),
        "s5_log_step": jax.random.uniform(ks[9], (DEPTH, S5_GROUPS), jnp.float32,
                                          minval=math.log(1e-3), maxval=math.log(1e-1)),
        "s5_b_re": nrm(ks[10], (DEPTH, S5_GROUPS, S5_STATE, S5_GROUP), jnp.float32) * (2 * S5_GROUP) ** -0.5,
        "s5_b_im": nrm(ks[11], (DEPTH, S5_GROUPS, S5_STATE, S5_GROUP), jnp.float32) * (2 * S5_GROUP) ** -0.5,
        "s5_c_re": 0.5 * nrm(ks[12], (DEPTH, S5_GROUPS, S5_GROUP, S5_STATE), jnp.float32),
        "s5_c_im": 0.5 * nrm(ks[13], (DEPTH, S5_GROUPS, S5_GROUP, S5_STATE), jnp.float32),
        "s5_d": nrm(ks[14], (DEPTH, S5_WIDTH), jnp.float32),
        "s5_w_glu": nrm(ks[15], (DEPTH, S5_WIDTH, S5_WIDTH), jnp.float32) * S5_WIDTH ** -0.5,
        "s5_b_glu": 0.02 * nrm(ks[16], (DEPTH, S5_WIDTH), jnp.float32),
        "ret_gn_w": 1.0 + 0.02 * nrm(ks[17], (DEPTH, RET_WIDTH), jnp.float32),
        "w_out": nrm(ks[18], (DEPTH, MIX_WIDTH, D_MODEL), jnp.float32) * MIX_WIDTH ** -0.5,
        "norm_ffn": 1.0 + 0.02 * nrm(ks[19], (DEPTH, D_MODEL), jnp.float32),
        "w_ffn_in": nrm(ks[20], (DEPTH, D_MODEL, 2 * D_FF), jnp.float32) * D_MODEL ** -0.5,
        "w_ffn_out": nrm(ks[21], (DEPTH, D_FF, D_MODEL), jnp.float32) * D_FF ** -0.5,
        "norm_final": 1.0 + 0.02 * nrm(ks[22], (D_MODEL,), jnp.float32),
    }


def reference(x_prompt, x_sample, state_s5_re, state_s5_im, state_ret,
              norm_mix, w_in, s5_lambda_re, s5_lambda_im, s5_log_step, s5_b_re, s5_b_im,
              s5_c_re, s5_c_im, s5_d, s5_w_glu, s5_b_glu, ret_gn_w, w_out, norm_ffn,
              w_ffn_in, w_ffn_out, norm_final):
    f32 = jnp.float32
    xp, xs = x_prompt, x_sample
    p_re, p_im, p_ret, s_re, s_im, s_ret = [], [], [], [], [], []
    zero_s5 = jnp.zeros((x_prompt.shape[0], S5_GROUPS, S5_STATE), f32)
    zero_ret = jnp.zeros((x_prompt.shape[0], RET_HEADS, RET_HEAD_DIM, RET_HEAD_DIM), f32)
    for li in range(DEPTH):
        weights = (norm_mix[li], w_in[li], s5_lambda_re[li], s5_lambda_im[li], s5_log_step[li],
                   s5_b_re[li], s5_b_im[li], s5_c_re[li], s5_c_im[li], s5_d[li], s5_w_glu[li],
                   s5_b_glu[li], ret_gn_w[li], w_out[li], norm_ffn[li], w_ffn_in[li], w_ffn_out[li])
        xp, pre, pim, pr = hybrid_layer(xp, zero_s5, zero_s5, zero_ret, jnp.float32(0.0), *weights)
        xs, sre, sim, sr = hybrid_layer(xs, state_s5_re[li], state_s5_im[li], state_ret[li],
                                        jnp.float32(PAST_LEN), *weights)
        p_re.append(pre); p_im.append(pim); p_ret.append(pr)
        s_re.append(sre); s_im.append(sim); s_ret.append(sr)
    y_prompt = rms_norm(xp, norm_final)
    y_sample = rms_norm(xs, norm_final)
    return (y_prompt, y_sample,
            jnp.stack(p_re), jnp.stack(p_im), jnp.stack(p_ret),
            jnp.stack(s_re), jnp.stack(s_im), jnp.stack(s_ret))
```

```python
import math
from contextlib import ExitStack
import numpy as np
import concourse.bass as bass
import concourse.mybir as mybir
from concourse.bass_utils import run_bass_kernel_spmd

F32 = mybir.dt.float32
BF16 = mybir.dt.bfloat16
AF = mybir.ActivationFunctionType
ALU = mybir.AluOpType

D = 2048
NT = 17
NR = 9
NTOK = NR * 128
DFF = 5632
EPS = 1e-6
GAMMA = [1.0 - 2.0 ** (-5.0 - h) for h in range(8)]


class TR:
    def __init__(self, nc, es):
        self.nc = nc
        self.es = es
        self.streams = {k: [] for k in ('pe', 'dve', 'act', 'pool', 'sp')}
        self.sems = {n: es.enter_context(nc.semaphore(n)) for n in ('pe', 'dve', 'act', 'pool')}
        self.cnt = {n: 0 for n in self.sems}
        self.waited = {}
        self.lw = {}
        self.rd = {}

    @staticmethod
    def _mul(s):
        return 16 if s.startswith('d_') else 1

    def op(self, stream, fn, reads=(), writes=(), dma=False):
        if dma:
            semn = 'd_' + writes[0]
            if semn not in self.sems:
                self.sems[semn] = self.es.enter_context(self.nc.semaphore(semn))
                self.cnt[semn] = 0
        else:
            semn = stream
        deps = {}

        def add(ev, raw):
            if ev is None:
                return
            s, n = ev
            if s == stream and s == 'pe':
                return
            deps[s] = max(deps.get(s, 0), n)

        if dma and self.cnt[semn]:
            deps[semn] = self.cnt[semn]
        for k in reads:
            add(self.lw.get(k), True)
            if k.startswith('ps'):
                for s, n in self.rd.get(k, {}).items():
                    if s != semn:
                        add((s, n), False)
        for k in writes:
            add(self.lw.get(k), False)
            for s, n in self.rd.get(k, {}).items():
                add((s, n), False)
        for s, n in deps.items():
            val = n * self._mul(s)
            if self.waited.get((stream, s), 0) >= val:
                continue
            self.waited[(stream, s)] = val
            self.streams[stream].append(('w', s, val))
        self.cnt[semn] += 1
        ev = (semn, self.cnt[semn])
        self.streams[stream].append(('o', fn, semn))
        for k in writes:
            self.lw[k] = ev
            self.rd[k] = {}
        for k in reads:
            d = self.rd.setdefault(k, {})
            d[semn] = max(d.get(semn, 0), ev[1])

    def barrier(self):
        for st in self.streams:
            for s_, c in self.cnt.items():
                if not c:
                    continue
                val = c * self._mul(s_)
                if self.waited.get((st, s_), 0) >= val:
                    continue
                self.waited[(st, s_)] = val
                self.streams[st].append(('w', s_, val))

    def finish(self):
        for s_, c in self.cnt.items():
            if c:
                val = c * self._mul(s_)
                if self.waited.get(('sp', s_), 0) >= val:
                    continue
                self.streams['sp'].append(('w', s_, val))

    def replay(self, stream, eng):
        for it in self.streams[stream]:
            if it[0] == 'w':
                eng.wait_ge(self.sems[it[1]], it[2])
            else:
                ins = it[1](eng)
                ins.then_inc(self.sems[it[2]], self._mul(it[2]))


import os
STOP_AT = int(os.environ.get('KSTOP', '99'))


def build():
    nc = bass.Bass("TRN2", target_bir_lowering=False)
    es = ExitStack()
    with es:
        def din(name, shape, dt=F32):
            return nc.dram_tensor(name, list(shape), dt, kind="ExternalInput").ap()

        def dout(name, shape):
            return nc.dram_tensor(name, list(shape), F32, kind="ExternalOutput").ap()

        xs = din("xs", [NT, 128, D])
        w_in = din("w_in", [D, 5120])
        w_out = din("w_out", [D, D])
        w_f1 = din("w_f1", [D, 2 * DFF])
        w_f2 = din("w_f2", [DFF, D])
        w_glu = din("w_glu", [1024, 1024])
        g3 = din("g3", [3, 128, D])
        gnw = din("gnw", [128, 1024])
        lam = din("lam", [128, 3, 32])
        btp = din("btp", [128, 32, 2, 128])
        ctp = din("ctp", [128, 32, 2, 128])
        dsm = din("dsm", [128, 8])
        bglu = din("bglu", [128, 8])
        s5i = din("s5i", [128, 16, 2, 32])
        r0 = din("r0", [16, 8, 128, 128])
        cqk = din("cqk", [NT, 128, 2, 64])
        sqk = din("sqk", [NT, 128, 2, 64])
        maskT = din("maskT", [2, 128, 8, 128])
        qdec = din("qdec", [2, 128, 8, 128])
        kdec = din("kdec", [2, 128, 8])
        bm01 = din("bm01", [128, 16, 128])
        rowm = din("rowm", [128, 16])
        idn = din("idn", [128, 128])

        y_o = dout("y", [NR, 128, D])
        s5p_o = dout("s5p", [128, 2, 32])
        s5s_o = dout("s5s", [128, 16, 2, 32])
        retp_o = dout("retp", [8, 128, 128])
        rets_o = dout("rets", [16, 8, 128, 128])

        T = TR(nc, es)
        psb = lambda name, shape, dt=F32: es.enter_context(nc.sbuf_tensor(name, list(shape), dt))
        ARN = 83 * 1024
        arena = psb("arena", [128, ARN], BF16)
        ar = {'off': 0}

        def areset():
            T.barrier()
            ar['off'] = 0

        def sb(name, shape, dt=F32):
            n = 1
            for d_ in shape[1:]:
                n *= d_
            w = n * (2 if dt == F32 else 1)
            w = (w + 15) // 16 * 16
            o = ar['off']
            assert o + w <= ARN, (name, o, w)
            ar['off'] = o + w
            v = arena[:, o:o + w]
            if dt == F32:
                v = v.bitcast(F32)
            v = v[:, 0:n]
            if len(shape) == 2:
                return v
            names = " ".join("d%d" % i for i in range(1, len(shape)))
            kw = {"d%d" % i: shape[i] for i in range(1, len(shape))}
            return v.rearrange("p (%s) -> p %s" % (names, names), **kw)
        ps = [es.enter_context(nc.psum_tensor("ps%d" % i, [128, 512], F32)) for i in range(8)]
        PK = ["ps%d" % i for i in range(8)]

        def dma(stream, out, in_, reads, writes):
            T.op(stream, lambda e: e.dma_start(out=out, in_=in_), reads, writes, dma=True)

        def act(out, in_, func, reads, writes, bias=None, scale=None, accum=None):
            kw = {}
            if bias is not None:
                kw['bias'] = bias
            if scale is not None:
                kw['scale'] = scale
            if accum is not None:
                kw['accum_out'] = accum
            T.op('act', lambda e: e.activation(out=out, in_=in_, func=func, **kw), reads, writes)

        def tt(out, a, b, op, reads, writes, eng='dve'):
            T.op(eng, lambda e: e.tensor_tensor(out=out, in0=a, in1=b, op=op), reads, writes)

        def ts(out, a, s1, s2, op0, op1, reads, writes, eng='dve'):
            if s2 is None:
                T.op(eng, lambda e: e.tensor_scalar(out=out, in0=a, scalar1=s1, scalar2=None, op0=op0), reads, writes)
            else:
                T.op(eng, lambda e: e.tensor_scalar(out=out, in0=a, scalar1=s1, scalar2=s2, op0=op0, op1=op1), reads, writes)

        def stt(out, a, s, b, op0, op1, reads, writes, eng='dve'):
            T.op(eng, lambda e: e.scalar_tensor_tensor(out=out, in0=a, scalar=s, in1=b, op0=op0, op1=op1), reads, writes)

        def cp(out, in_, reads, writes, eng='dve'):
            if eng == 'act':
                T.op(eng, lambda e: e.activation(out=out, in_=in_, func=AF.Copy), reads, writes)
            else:
                T.op(eng, lambda e: e.tensor_copy(out=out, in_=in_), reads, writes)

        def mm(out, lhsT, rhs, start, stop, reads, writes):
            T.op('pe', lambda e: e.matmul(out, lhsT, rhs, start=start, stop=stop), reads, writes)

        def tp(out, in_, ident, reads, writes):
            T.op('pe', lambda e: e.transpose(out, in_, ident), reads, writes)

        def memset(ap, v, writes, eng='dve'):
            T.op(eng, lambda e: e.memset(ap, v), (), writes)

        def _phases():
            ident_f = psb("ident_f", [128, 128])
            ident = psb("ident", [128, 128], BF16)
            dma('sp', ident_f[:], idn, (), ['ident_f'])
            cp(ident[:], ident_f[:], ['ident_f'], ['ident'])
            small = psb("small", [128, 64])
            lam_t = psb("lam_t", [128, 3, 32])
            dma('sp', lam_t[:], lam, (), ['lam'])
            dsm_t = psb("dsm_t", [128, 8])
            bglu_t = psb("bglu_t", [128, 8])
            dma('sp', dsm_t[:], dsm, (), ['dsm'])
            dma('sp', bglu_t[:], bglu, (), ['bglu'])
            kdec_t = psb("kdec_t", [128, 2, 8])
            for i in range(2):
                dma('sp', kdec_t[:, i, :], kdec[i], (), ['kdec'])
            rowm_t = psb("rowm_t", [128, 16])
            dma('sp', rowm_t[:], rowm, (), ['rowm'])

            cf = psb("cf", [128, 16, 32])
            DT, A_, TH, EA, U0, F1, SN, CS, LR, LI, KR, KI, T0, T1, DEN, LIN = [cf[:, i, :] for i in range(16)]
            cf2 = psb("cf2", [128, 4, 32])
            IKR, IKI, T2, T3 = [cf2[:, i, :] for i in range(4)]
            C = ['cf']
            act(DT, lam_t[:, 2, :], AF.Exp, ['lam'], C)
            tt(A_, lam_t[:, 0, :], DT, ALU.mult, ['lam'] + C, C)
            tt(TH, lam_t[:, 1, :], DT, ALU.mult, ['lam'] + C, C)
            act(EA, A_, AF.Exp, C, C)
            ts(U0, TH, 1.0 / 16.0, None, ALU.mult, None, C, C)
            act(SN, U0, AF.Sin, C, C)
            ts(F1, U0, math.pi / 2, None, ALU.add, None, C, C)
            act(CS, F1, AF.Sin, C, C)
            for _ in range(4):
                tt(T0, SN, CS, ALU.mult, C, C)
                tt(T1, CS, CS, ALU.mult, C, C)
                tt(F1, SN, SN, ALU.mult, C, C)
                ts(SN, T0, 2.0, None, ALU.mult, None, C, C)
                tt(CS, T1, F1, ALU.subtract, C, C)
            tt(LR, EA, CS, ALU.mult, C, C)
            tt(LI, EA, SN, ALU.mult, C, C)
            ts(LIN, LI, -1.0, None, ALU.mult, None, C, C)
            ts(T0, LR, -1.0, None, ALU.add, None, C, C)
            tt(T1, lam_t[:, 0, :], lam_t[:, 0, :], ALU.mult, ['lam'] + C, C)
            tt(DEN, lam_t[:, 1, :], lam_t[:, 1, :], ALU.mult, ['lam'] + C, C)
            tt(DEN, DEN, T1, ALU.add, C, C)
            T.op('dve', lambda e: e.reciprocal(out=DEN, in_=DEN), C, C)
            tt(KR, T0, lam_t[:, 0, :], ALU.mult, ['lam'] + C, C)
            tt(T1, LI, lam_t[:, 1, :], ALU.mult, ['lam'] + C, C)
            tt(KR, KR, T1, ALU.add, C, C)
            tt(KR, KR, DEN, ALU.mult, C, C)
            tt(KI, LI, lam_t[:, 0, :], ALU.mult, ['lam'] + C, C)
            tt(T1, T0, lam_t[:, 1, :], ALU.mult, ['lam'] + C, C)
            tt(KI, KI, T1, ALU.subtract, C, C)
            tt(KI, KI, DEN, ALU.mult, C, C)
            C2 = ['cf2']
            tt(T2, KR, KR, ALU.mult, C, C2)
            tt(T3, KI, KI, ALU.mult, C, C2)
            tt(T2, T2, T3, ALU.add, C2, C2)
            T.op('dve', lambda e: e.reciprocal(out=T2, in_=T2), C2, C2)
            tt(IKR, KR, T2, ALU.mult, C + C2, C2)
            tt(IKI, KI, T2, ALU.mult, C + C2, C2)
            ts(IKI, IKI, -1.0, None, ALU.mult, None, C2, C2)
            LL = psb("LL", [128, 2, 2, 32])
            cp(LL[:, 0, 0, :], LR, C, ['LL'])
            cp(LL[:, 0, 1, :], LR, C, ['LL'])
            cp(LL[:, 1, 0, :], LIN, C, ['LL'])
            cp(LL[:, 1, 1, :], LI, C, ['LL'])

            def cmul_bc(out_re, out_im, a_re, a_im, cr, ci, tmp, rk, wk, neg_im=False):
                tt(out_re, a_re, cr, ALU.mult, rk, wk)
                tt(tmp, a_im, ci, ALU.mult, rk, wk)
                tt(out_re, out_re, tmp, ALU.subtract, wk, wk)
                tt(out_im, a_re, ci, ALU.mult, rk, wk)
                tt(tmp, a_im, cr, ALU.mult, rk, wk)
                tt(out_im, out_im, tmp, ALU.add, wk, wk)
                if neg_im:
                    ts(out_im, out_im, -1.0, None, ALU.mult, None, wk, wk)

            mixT = psb("mixT", [128, 16, NTOK], BF16)
            hT_d = nc.dram_tensor("hT_d", [NT, 128, 16, 128], BF16).ap()

            if STOP_AT <= 1:
                return
            def norm_s1(x_t, xk, gam_ap, hb, hbk, par):
                c0 = 2 * par
                sk0, sk1 = 'small%da' % par, 'small%db' % par
                act(hb, x_t, AF.Square, [xk], [hbk, sk0], accum=small[:, c0:c0 + 1])
                ts(small[:, c0 + 1:c0 + 2], small[:, c0:c0 + 1], 1.0 / D, EPS, ALU.mult, ALU.add, [sk0], [sk1])
                act(small[:, c0 + 1:c0 + 2], small[:, c0 + 1:c0 + 2], AF.Sqrt, [sk1], [sk1])
                T.op('dve', lambda e: e.reciprocal(out=small[:, c0 + 1:c0 + 2], in_=small[:, c0 + 1:c0 + 2]), [sk1], [sk1])
                stt(hb, x_t, small[:, c0 + 1:c0 + 2], gam_ap, ALU.mult, ALU.mult, [xk, sk1, 'gam'], [hbk])

            def norm_s2(outT, outk, hb, hbk):
                for half in range(2):
                    pt = ps[6 + half]
                    ptb = pt[:].bitcast(BF16)
                    for kk in range(8):
                        k = half * 8 + kk
                        tp(ptb[:, kk * 128:(kk + 1) * 128], hb[:, k * 128:(k + 1) * 128], ident[:],
                           [hbk, 'ident'], [PK[6 + half]])
                    cp(outT[:, half * 8:(half + 1) * 8, :],
                       ptb[:, 0:1024].rearrange("p (k t) -> p k t", k=8), [PK[6 + half]], [outk],
                       eng='act' if half else 'dve')

            gam = sb("gam", [128, D])
            dma('sp', gam[:], g3[0], (), ['gam'])
            xa = [sb("xa%d" % i, [128, D]) for i in range(2)]
            hb = [sb("hb%d" % i, [128, D], BF16) for i in range(2)]
            hTt = [sb("hTt%d" % i, [128, 16, 128], BF16) for i in range(2)]
            def a_s1(t):
                b = t % 2
                dma('sp', xa[b][:], xs[t], (), ['xa%d' % b])
                norm_s1(xa[b][:], 'xa%d' % b, gam[:], hb[b][:], 'hb%d' % b, b)

            a_s1(0)
            for t in range(NT):
                b = t % 2
                if t + 1 < NT:
                    a_s1(t + 1)
                norm_s2(hTt[b], 'hTt%d' % b, hb[b][:], 'hb%d' % b)
                dma('sp', hT_d[t], hTt[b][:], ['hTt%d' % b], ['hT_d%d' % b])

            if STOP_AT <= 2:
                return
            areset()
            uT = sb("uT", [128, 8, NT * 128], BF16)
            mark1 = ar['off']
            hblk = [sb("hblk%d" % i, [128, 4, 16, 128], BF16) for i in range(2)]
            wblk = [sb("wblk%d" % i, [128, 16, 128], BF16) for i in range(8)]
            for ct in range(8):
                dma('pool', wblk[ct][:], w_in[:, ct * 128:(ct + 1) * 128].rearrange("(k p) c -> p k c", p=128),
                    (), ['wblk%d' % ct])
            n = 0
            for blk in range(5):
                t0 = blk * 4
                nt = min(4, NT - t0)
                hbk = 'hblk%d' % (blk % 2)
                dma('sp', hblk[blk % 2][:, 0:nt], hT_d[t0:t0 + nt].rearrange("t p k c -> p t k c"), ['hT_d0', 'hT_d1'], [hbk])
                for ct in range(8):
                    pb = n % 2
                    n += 1
                    for k in range(16):
                        mm(ps[pb][:, 0:nt * 128], wblk[ct][:, k, :],
                           hblk[blk % 2][:, 0:nt, k, :], k == 0, k == 15, ['wblk%d' % ct, hbk], [PK[pb]])
                    cp(uT[:, ct, t0 * 128:(t0 + nt) * 128], ps[pb][:, 0:nt * 128], [PK[pb]], ['uT'],
                       eng='act' if n % 2 else 'dve')

            if STOP_AT <= 3:
                return
            T.barrier()
            ar['off'] = mark1
            Bv = sb("Bv", [128, 32, 2, 128], BF16)
            dma('pool', Bv[:], btp, (), ['Bv'])
            Cv = sb("Cv", [128, 32, 2, 128], BF16)
            mark2 = ar['off']
            ctf = sb("ctf", [128, 8, 2, 128])
            cto = sb("cto", [128, 8, 3, 128])
            for g in range(4):
                dma('sp', ctf[:], ctp[:, g * 8:(g + 1) * 8], (), ['ctf'])
                krb = KR[:, g * 8:(g + 1) * 8].unsqueeze(2).broadcast_to([128, 8, 128])
                kib = KI[:, g * 8:(g + 1) * 8].unsqueeze(2).broadcast_to([128, 8, 128])
                cmul_bc(cto[:, :, 0, :], cto[:, :, 1, :], ctf[:, :, 0, :], ctf[:, :, 1, :], krb, kib,
                        cto[:, :, 2, :], ['ctf'] + C, ['cto'], neg_im=True)
                cp(Cv[:, g * 8:(g + 1) * 8, 0, :], cto[:, :, 0, :], ['cto'], ['Cv'])
                cp(Cv[:, g * 8:(g + 1) * 8, 1, :], cto[:, :, 1, :], ['cto'], ['Cv'])
            T.barrier()
            ar['off'] = mark2
            TC = 64
            Ec = sb("Ec", [128, 32, TC])
            Es = sb("Es", [128, 32, TC])
            Rt = sb("Rt", [128, 32, TC])
            wzb = [sb("wz%d" % i, [128, 2, 32, TC]) for i in range(2)]
            Xb = sb("Xb", [128, 2, 32, TC], BF16)
            rt1 = sb("rt1", [128, 32, TC])
            etmp = rt1
            rt2 = sb("rt2", [128, 32, TC])
            carry = sb("carry", [128, 8, 2, 32])
            ctmp = sb("ctmp", [128, 8, 2, 32])
            ctm2 = sb("ctm2", [128, 32, 8])
            ysb = sb("ysb", [128, 8, TC])
            yt1 = sb("yt1", [128, 8, TC])
            fin = sb("fin", [128, 16, 2, 32])
            ftmp = sb("ftmp", [128, 16, 32])
            s5i_t = sb("s5i_t", [128, 16, 2, 32])
            EK = ['E']
            cp(Ec[:, :, 0], CS, C, EK)
            cp(Es[:, :, 0], SN, C, EK)
            ln = 1
            while ln < TC:
                cmul_bc(Ec[:, :, ln:2 * ln], Es[:, :, ln:2 * ln], Ec[:, :, 0:ln], Es[:, :, 0:ln],
                        Ec[:, :, ln - 1:ln].broadcast_to([128, 32, ln]), Es[:, :, ln - 1:ln].broadcast_to([128, 32, ln]),
                        etmp[:, :, 0:ln], EK, EK)
                ln *= 2
            cp(Rt[:], EA.unsqueeze(2).broadcast_to([128, 32, TC]), C, ['Rt'])
            memset(Rt[:, :, 0:1], 0.0, ['Rt'])
            memset(carry[:], 0.0, ['carry'])

            def s5_parts(t, hf, nseq, real_idx, par):
                steps = TC // nseq
                tok0 = t * 128 + hf * TC
                wz = wzb[par]
                WK = 'wz%d' % par

                def A_pe():
                    for g in range(4):
                        bre = ps[(g % 2) * 2]
                        bim = ps[(g % 2) * 2 + 1]
                        kre = PK[(g % 2) * 2]
                        kim = PK[(g % 2) * 2 + 1]
                        for jj in range(8):
                            j = g * 8 + jj
                            mm(bre[:, jj * TC:(jj + 1) * TC], Bv[:, j, 0, :], uT[:, j // 4, tok0:tok0 + TC], True, True,
                               ['Bv', 'uT'], [kre])
                        for jj in range(8):
                            j = g * 8 + jj
                            mm(bim[:, jj * TC:(jj + 1) * TC], Bv[:, j, 1, :], uT[:, j // 4, tok0:tok0 + TC], True, True,
                               ['Bv', 'uT'], [kim])
                        act(wz[:, 0, g * 8:(g + 1) * 8, :], bre[:, :].rearrange("p (j t) -> p j t", j=8), AF.Copy,
                            [kre], [WK])
                        act(wz[:, 1, g * 8:(g + 1) * 8, :], bim[:, :].rearrange("p (j t) -> p j t", j=8), AF.Copy,
                            [kim], [WK])

                def FWD():
                    W0, W1 = WK + 'r', WK + 'i'
                    FE = os.environ.get('KFWD', 'pool')
                    tt(rt1[:], wz[:, 1], Es[:], ALU.mult, [WK, W1, 'E'], ['rt1'])
                    tt(rt2[:], wz[:, 0], Es[:], ALU.mult, [WK, W0, 'E'], ['rt2'], eng=FE)
                    tt(wz[:, 0], wz[:, 0], Ec[:], ALU.mult, [WK, W0, 'E'], [W0])
                    tt(wz[:, 0], wz[:, 0], rt1[:], ALU.add, [W0, 'rt1'], [W0])
                    tt(wz[:, 1], wz[:, 1], Ec[:], ALU.mult, [WK, W1, 'E'], [W1], eng=FE)
                    tt(wz[:, 1], wz[:, 1], rt2[:], ALU.subtract, [W1, 'rt2'], [W1, WK], eng=FE)

                def Bp():
                    tt(ctmp[:, 0:nseq], carry[:, 0:nseq],
                       EA.unsqueeze(1).unsqueeze(1).broadcast_to([128, nseq, 2, 32]), ALU.mult, ['carry'] + C, ['ctmp'])
                    wfirst = wz[:, :, :, 0::steps]
                    tt(wfirst, wfirst, ctmp[:, 0:nseq].rearrange("p s r j -> p r j s"), ALU.add,
                       [WK, WK + 'r', 'ctmp'], [WK, WK + 'r'])
                    for r in range(2):
                        zf = wz[:, r].rearrange("p j t -> p (j t)")
                        T.op('dve', lambda e, zf=zf: e.tensor_tensor_scan(
                            out=zf, data0=Rt[:].rearrange("p j t -> p (j t)"), data1=zf, initial=0.0,
                            op0=ALU.mult, op1=ALU.add), [WK, 'Rt'], [WK])
                    cv = carry[:, 0:nseq].rearrange("p s r j -> p r j s")
                    cmul_bc(cv[:, 0], cv[:, 1], wz[:, 0, :, steps - 1::steps], wz[:, 1, :, steps - 1::steps],
                            Ec[:, :, steps - 1:steps].broadcast_to([128, 32, nseq]),
                            Es[:, :, steps - 1:steps].broadcast_to([128, 32, nseq]),
                            ctm2[:, :, 0:nseq], [WK, 'E'], ['carry'])
                    if real_idx is None:
                        return
                    tt(rt1[:], wz[:, 0], Ec[:], ALU.mult, [WK, 'E'], ['rt1'])
                    tt(rt2[:], wz[:, 1], Es[:], ALU.mult, [WK, 'E'], ['rt2'])
                    tt(Xb[:, 0], rt1[:], rt2[:], ALU.subtract, ['rt1', 'rt2'], ['Xb'])
                    tt(rt1[:], wz[:, 0], Es[:], ALU.mult, [WK, 'E'], ['rt1'])
                    tt(rt2[:], wz[:, 1], Ec[:], ALU.mult, [WK, 'E'], ['rt2'])
                    tt(Xb[:, 1], rt1[:], rt2[:], ALU.add, ['rt1', 'rt2'], ['Xb'])
                    for ct in range(8):
                        n = 0
                        for q in range(4):
                            j = ct * 4 + q
                            for r in range(2):
                                mm(ps[4][:, ct * TC:(ct + 1) * TC], Cv[:, j, r, :], Xb[:, r, j, :],
                                   n == 0, n == 7, ['Cv', 'Xb'], [PK[4]])
                                n += 1

                def Cp():
                    if real_idx is None:
                        return
                    tt(yt1[:], uT[:, :, tok0:tok0 + TC], dsm_t[:].unsqueeze(2).broadcast_to([128, 8, TC]), ALU.mult,
                       ['uT', 'dsm'], ['yt1'])
                    tt(ysb[:], yt1[:], ps[4][:, :].rearrange("p (c t) -> p c t", c=8), ALU.add, ['yt1', PK[4]], ['ysb'])
                    tt(yt1[:], ysb[:], ysb[:], ALU.mult, ['ysb'], ['yt1'])
                    ts(yt1[:], yt1[:], 0.044715, 1.0, ALU.mult, ALU.add, ['yt1'], ['yt1'])
                    tt(yt1[:], yt1[:], ysb[:], ALU.mult, ['yt1', 'ysb'], ['yt1'])
                    act(yt1[:], yt1[:], AF.Sigmoid, ['yt1'], ['yt1'], scale=2.0 * math.sqrt(2.0 / math.pi))
                    c0 = real_idx * 128 + hf * TC
                    tt(mixT[:, 0:8, c0:c0 + TC], ysb[:], yt1[:], ALU.mult, ['ysb', 'yt1'], ['mixS'])

                return A_pe, FWD, Bp, Cp

            halves = [s5_parts(t, hf, 1, (t - 8) if t >= 8 else None, (2 * t + hf) % 2)
                      for t in range(16) for hf in range(2)]
            halves[0][0]()
            halves[0][1]()
            for n_ in range(len(halves)):
                if n_ + 1 < len(halves):
                    halves[n_ + 1][0]()
                halves[n_][2]()
                if n_ + 1 < len(halves):
                    halves[n_ + 1][1]()
                halves[n_][3]()

            def s5_half(t, hf, nseq, real_idx):
                A_pe, FWD, Bp, Cp = s5_parts(t, hf, nseq, real_idx, hf)
                A_pe(); FWD(); Bp(); Cp()

            cmul_bc(fin[:, 0:1, 0, :], fin[:, 0:1, 1, :], carry[:, 0:1, 0, :], carry[:, 0:1, 1, :],
                    KR.unsqueeze(1), KI.unsqueeze(1), ftmp[:, 0:1, :], ['carry'] + C, ['fin'])
            dma('sp', s5p_o, fin[:, 0], ['fin'], ['s5p_o'])
            dma('sp', s5i_t[:], s5i, (), ['s5i'])
            for tb in (Ec, Es):
                cp(tb[:, :, 8:TC].rearrange("p j (s t) -> p j s t", t=8),
                   tb[:, :, 0:8].unsqueeze(2).broadcast_to([128, 32, TC // 8 - 1, 8]), EK, EK)
            memset(Rt[:, :, 8::8], 0.0, ['Rt'])
            for hf in range(2):
                sl = slice(hf * 8, hf * 8 + 8)
                cmul_bc(carry[:, :, 0, :], carry[:, :, 1, :], s5i_t[:, sl, 0, :], s5i_t[:, sl, 1, :],
                        IKR.unsqueeze(1).broadcast_to([128, 8, 32]), IKI.unsqueeze(1).broadcast_to([128, 8, 32]),
                        ftmp[:, 0:8, :], ['s5i'] + C2, ['carry'])
                s5_half(16, hf, 8, 8)
                cmul_bc(fin[:, sl, 0, :], fin[:, sl, 1, :], carry[:, :, 0, :], carry[:, :, 1, :],
                        KR.unsqueeze(1).broadcast_to([128, 8, 32]), KI.unsqueeze(1).broadcast_to([128, 8, 32]),
                        ftmp[:, 0:8, :], ['carry'] + C, ['fin'])
            dma('sp', s5s_o, fin[:], ['fin'], ['s5s_o'])

            if STOP_AT <= 4:
                return
            areset()
            wg = sb("wg", [128, 8, 1024], BF16)
            dma('pool', wg[:], w_glu.rearrange("(k p) c -> p k c", p=128), (), ['wg'])
            glu_o = sb("glu_o", [128, 8, NTOK], BF16)
            sg_t = [sb("sg_t%d" % i, [128, 512]) for i in range(2)]
            blocks = [(0, 512), (512, 512), (1024, 128)]
            n = 0
            for m in range(8):
                for (c0, cn) in blocks:
                    pb = n % 2
                    for k in range(8):
                        mm(ps[pb][:, 0:cn], wg[:, k, m * 128:(m + 1) * 128], mixT[:, k, c0:c0 + cn], k == 0, k == 7,
                           ['wg', 'mixS'], [PK[pb]])
                    act(sg_t[pb][:, 0:cn], ps[pb][:, 0:cn], AF.Sigmoid, [PK[pb], 'bglu'], ['sg_t%d' % pb],
                        bias=bglu_t[:, m:m + 1])
                    tt(glu_o[:, m, c0:c0 + cn], mixT[:, m, c0:c0 + cn], sg_t[pb][:, 0:cn], ALU.mult,
                       ['mixS', 'sg_t%d' % pb], ['glu_o'])
                    n += 1
            cp(mixT[:, 0:8, :], glu_o[:], ['glu_o', 'mixS'], ['mixS'], eng='pool')

            if STOP_AT <= 5:
                return
            areset()
            hTc = [sb("hTc%d" % i, [128, 16, 128], BF16) for i in range(2)]
            gnw_t = sb("gnw_t", [128, 1024])
            dma('sp', gnw_t[:], gnw, (), ['gnw'])
            cqk_t = sb("cqk_t", [128, NT, 2, 64])
            sqk_t = sb("sqk_t", [128, NT, 2, 64])
            dma('sp', cqk_t[:], cqk.rearrange("t p a f -> p t a f"), (), ['rot'])
            dma('sp', sqk_t[:], sqk.rearrange("t p a f -> p t a f"), (), ['rot2'])
            eps_t = sb("eps_t", [128, 1])
            memset(eps_t[:], EPS, ['eps_t'])
            maskT_t = sb("maskT_t", [128, 2, 8, 128])
            qdec_t = sb("qdec_t", [128, 2, 8, 128])
            for i in range(2):
                dma('sp', maskT_t[:, i], maskT[i], (), ['maskT'])
                dma('sp', qdec_t[:, i], qdec[i], (), ['qdec'])
            bm01_t = sb("bm01_t", [128, 16, 128])
            dma('sp', bm01_t[:], bm01, (), ['bm01'])
            wh = [sb("wh%d" % i, [128, 16, 512], BF16) for i in range(2)]
            R = sb("R", [128, 128])
            Rb = [sb("Rb%d" % i, [128, 128], BF16) for i in range(2)]

            def two(name, shape, dt=F32):
                return [sb("%s%d" % (name, i), shape, dt) for i in range(2)]
            qk = two("qk", [128, 2, 128], BF16)
            rA = two("rA", [128, 2, 2, 64]); rB = two("rB", [128, 2, 2, 64])
            qkT = two("qkT", [128, 256], BF16)
            vsb = two("vsb", [128, 128], BF16); sgate = two("sgate", [128, 128])
            kd = two("kd", [128, 128], BF16)
            qT = two("qT", [128, 128], BF16); qdT = two("qdT", [128, 128], BF16); kT = two("kT", [128, 128], BF16)
            sT = two("sT", [128, 128], BF16)
            rt = two("rt", [128, 4, 64])
            rtq = two("rtq", [128, 4, 64])
            osb = two("osb", [128, 128]); osq = two("osq", [128, 128])
            ro = two("ro", [128, 128], BF16)
            st = two("st", [128, 8])
            qm = sb("qm", [128, 16, 128], BF16)
            km = sb("km", [128, 16, 128], BF16)
            r0f = sb("r0f", [128, 16, 128])
            r0b = sb("r0b", [128, 16, 128], BF16)

            def rotary(out, src, c, s, rk, wk, rtb, rtk):
                x1 = src[:, 0:64]; x2 = src[:, 64:128]
                tt(rtb[:, 0, :], x1, c, ALU.mult, rk, [rtk])
                tt(rtb[:, 1, :], x2, s, ALU.mult, rk, [rtk])
                tt(out[:, 0:64], rtb[:, 0, :], rtb[:, 1, :], ALU.subtract, [rtk], wk)
                tt(rtb[:, 2, :], x1, s, ALU.mult, rk, [rtk])
                tt(rtb[:, 3, :], x2, c, ALU.mult, rk, [rtk])
                tt(out[:, 64:128], rtb[:, 2, :], rtb[:, 3, :], ALU.add, [rtk], wk)

            eg = two("eg", [128, 128])
            NTT = int(os.environ.get('KRT', str(NT)))

            def load_wh(h):
                for ci, base in enumerate((1024, 2048, 3072, 4096)):
                    c0 = base + h * 128
                    dma('pool', wh[h % 2][:, :, ci * 128:(ci + 1) * 128],
                        w_in[:, c0:c0 + 128].rearrange("(k p) c -> p k c", p=128), (), ['wh%d' % (h % 2)])

            NHD = int(os.environ.get('KRH', '8'))
            load_wh(0)

            def ret_head(h):
                b = h % 2
                if h + 1 < NHD:
                    load_wh(h + 1)
                dma('sp', r0f[:], r0[:, h].rearrange("s d e -> d s e"), (), ['r0f'])
                cp(r0b[:], r0f[:], ['r0f'], ['r0b'], eng='pool')
                memset(R[:], 0.0, ['R'])
                memset(Rb[0][:], 0.0, ['Rb0'])

                def info(t):
                    p = t % 2
                    return p, str(p), (t >= 8), (t == 16), (1 if t == 16 else 0)

                def P0(t, half):
                    p, P, real, smp, mi = info(t)
                    ncol = 512 if real else 256
                    if half == 0:
                        dma('sp', hTc[p][:], hT_d[t], ['hT_d0', 'hT_d1'], ['hTc' + P])
                    w0 = 0 if real else 128
                    for k in range(half * 8, half * 8 + 8):
                        mm(ps[p][:, 0:ncol], hTc[p][:, k, :], wh[b][:, k, w0:w0 + ncol], k == 0, k == 15,
                           ['hTc' + P, 'wh%d' % b], [PK[p]])

                def D0(t):
                    p, P, real, smp, mi = info(t)
                    pp = ps[p]; ppk = PK[p]
                    if real:
                        n2, a0, vcol, src = 2, 0, 256, pp[:, 0:256]
                    else:
                        n2, a0, vcol, src = 1, 1, 128, pp[:, 0:128]
                    sv = src.rearrange("p (a h f) -> p a h f", a=n2, h=2)
                    ct_ = cqk_t[:, t, a0:a0 + n2, :]
                    st_ = sqk_t[:, t, a0:a0 + n2, :]
                    A = rA[p][:, 0:n2]; Bt = rB[p][:, 0:n2]
                    out = qk[p][:, a0:a0 + n2, :].rearrange("p a (h f) -> p a h f", h=2)
                    tt(A, sv, ct_.unsqueeze(2).broadcast_to([128, n2, 2, 64]), ALU.mult, [ppk, 'rot'], ['rA' + P])
                    tt(Bt[:, :, 0, :], sv[:, :, 1, :], st_, ALU.mult, [ppk, 'rot2'], ['rB' + P])
                    tt(Bt[:, :, 1, :], sv[:, :, 0, :], st_, ALU.mult, [ppk, 'rot2'], ['rB' + P])
                    tt(out[:, :, 0, :], A[:, :, 0, :], Bt[:, :, 0, :], ALU.subtract, ['rA' + P, 'rB' + P], ['qk' + P])
                    tt(out[:, :, 1, :], A[:, :, 1, :], Bt[:, :, 1, :], ALU.add, ['rA' + P, 'rB' + P], ['qk' + P])
                    act(vsb[p][:], pp[:, vcol:vcol + 128], AF.Copy, [ppk], ['vsb' + P])
                    ts(kd[p][:], qk[p][:, 1, :], kdec_t[:, mi, h:h + 1], None, ALU.mult, None,
                       ['qk' + P, 'kdec'], ['kd' + P])
                    if real:
                        act(eg[p][:], pp[:, 384:512], AF.Exp, [ppk], ['eg' + P], scale=-1.0)
                        ts(eg[p][:], eg[p][:], 1.0, None, ALU.add, None, ['eg' + P], ['eg' + P])
                        T.op('dve', lambda e, x=eg[p]: e.reciprocal(out=x[:], in_=x[:]), ['eg' + P], ['eg' + P])
                        tt(sgate[p][:], pp[:, 384:512], eg[p][:], ALU.mult, [ppk, 'eg' + P], ['sgate' + P])

                def S(t):
                    p, P, real, smp, mi = info(t)
                    if not smp:
                        p5 = ps[6 + p]; k5 = PK[6 + p]
                        mm(p5[:, 0:128], kd[p][:], vsb[p][:], True, True, ['kd' + P, 'vsb' + P], [k5])
                        stt(R[:], R[:], GAMMA[h] ** 128, p5[:, 0:128], ALU.mult, ALU.add, ['R', k5], ['R'])
                        cp(Rb[1 - p][:], R[:], ['R'], ['Rb%d' % (1 - p)], eng='pool')
                        if t == 15:
                            dma('sp', retp_o[h], R[:], ['R'], ['retp_o'])
                    else:
                        tt(km[:], kd[p][:].unsqueeze(1).broadcast_to([128, 16, 128]),
                           rowm_t[:].unsqueeze(2).broadcast_to([128, 16, 128]), ALU.mult, ['kd' + P, 'rowm'], ['km'])
                        for s_ in range(16):
                            bank = 4 + s_ // 4
                            mm(ps[bank][:, (s_ % 4) * 128:(s_ % 4 + 1) * 128], km[:, s_, :], vsb[p][:], True, True,
                               ['km', 'vsb' + P], [PK[bank]])
                        for b4 in range(4):
                            stt(r0f[:, b4 * 4:(b4 + 1) * 4, :], r0f[:, b4 * 4:(b4 + 1) * 4, :], GAMMA[h] ** 8,
                                ps[4 + b4][:, :].rearrange("p (s e) -> p s e", s=4), ALU.mult, ALU.add,
                                ['r0f', PK[4 + b4]], ['r0f'])
                        dma('sp', rets_o[:, h].rearrange("s d e -> d s e"), r0f[:], ['r0f'], ['rets_o'])

                def P1D1(t):
                    p, P, real, smp, mi = info(t)
                    p2 = ps[2 + p][:].bitcast(BF16); k2 = PK[2 + p]
                    tp(p2[:, 0:128], qk[p][:, 0, :], ident[:], ['qk' + P, 'ident'], [k2])
                    tp(p2[:, 128:256], qk[p][:, 1, :], ident[:], ['qk' + P, 'ident'], [k2])
                    cp(qkT[p][:], p2[:, 0:256], [k2], ['qkT' + P], eng='act')
                    tt(qdT[p][:], p2[:, 0:128], qdec_t[:, mi, h, :], ALU.mult, [k2, 'qdec'], ['qdT' + P])

                def P2D2(t):
                    p, P, real, smp, mi = info(t)
                    p3 = ps[4 + p]; k3 = PK[4 + p]
                    mm(p3[:, 0:128], qkT[p][:, 128:256], qkT[p][:, 0:128], True, True, ['qkT' + P], [k3])
                    tt(sT[p][:], p3[:, 0:128], maskT_t[:, mi, h, :], ALU.mult, [k3, 'maskT'], ['sT' + P])
                    if smp:
                        tt(qm[:], qdT[p][:].unsqueeze(1).broadcast_to([128, 16, 128]), bm01_t[:], ALU.mult,
                           ['qdT' + P, 'bm01'], ['qm'])

                def P3(t):
                    p, P, real, smp, mi = info(t)
                    p3 = ps[4 + p]; k3 = PK[4 + p]
                    mm(p3[:, 128:256], sT[p][:], vsb[p][:], True, False, ['sT' + P, 'vsb' + P], [k3])
                    if smp:
                        for s_ in range(16):
                            mm(p3[:, 128:256], qm[:, s_, :], r0b[:, s_, :], False, s_ == 15, ['qm', 'r0b'], [k3])
                    else:
                        mm(p3[:, 128:256], qdT[p][:], Rb[p][:], False, True, ['qdT' + P, 'Rb' + P], [k3])

                def D3(t):
                    p, P, real, smp, mi = info(t)
                    p3 = ps[4 + p]; k3 = PK[4 + p]
                    o_ps = p3[:, 128:256]
                    stp = st[p]
                    act(osb[p][:], o_ps, AF.Copy, [k3], ['osb' + P, 'sta' + P], accum=stp[:, 0:1])
                    act(osq[p][:], o_ps, AF.Square, [k3], ['osq' + P, 'stb' + P], accum=stp[:, 1:2])
                    ts(stp[:, 2:3], stp[:, 0:1], 1.0 / 128, None, ALU.mult, None, ['sta' + P], ['stc' + P])
                    stt(stp[:, 3:4], stp[:, 2:3], -1.0, stp[:, 2:3], ALU.mult, ALU.mult, ['stc' + P], ['std' + P])
                    stt(stp[:, 4:5], stp[:, 1:2], 1.0 / 128, stp[:, 3:4], ALU.mult, ALU.add,
                        ['stb' + P, 'std' + P], ['ste' + P])
                    act(stp[:, 5:6], stp[:, 4:5], AF.Ln, ['ste' + P, 'eps_t'], ['stf' + P], bias=eps_t[:, 0:1])
                    act(stp[:, 5:6], stp[:, 5:6], AF.Exp, ['stf' + P], ['stf' + P], scale=-0.5)
                    ts(osb[p][:], osb[p][:], stp[:, 2:3], stp[:, 5:6], ALU.subtract, ALU.mult,
                       ['osb' + P, 'stc' + P, 'stf' + P], ['osb' + P])
                    tt(osb[p][:], osb[p][:], gnw_t[:, h * 128:(h + 1) * 128], ALU.mult, ['osb' + P, 'gnw'], ['osb' + P])
                    tt(ro[p][:], osb[p][:], sgate[p][:], ALU.mult, ['osb' + P, 'sgate' + P], ['ro' + P])

                def P4D4(t):
                    p, P, real, smp, mi = info(t)
                    p2 = ps[2 + p][:].bitcast(BF16); k2 = PK[2 + p]
                    tp(p2[:, 256:384], ro[p][:], ident[:], ['ro' + P, 'ident'], [k2])
                    ri = t - 8
                    cp(mixT[:, 8 + h, ri * 128:(ri + 1) * 128], p2[:, 256:384], [k2], ['mixR'], eng='act')

                def valid(t):
                    return 0 <= t < NTT

                def realv(t):
                    return valid(t) and t >= 8

                for i in range(NTT + 2):
                    if realv(i - 2):
                        D3(i - 2)
                    if valid(i - 1):
                        S(i - 1)
                    if realv(i - 1):
                        P1D1(i - 1)
                    if valid(i):
                        P0(i, 0)
                    if realv(i - 1):
                        P2D2(i - 1)
                    if valid(i):
                        P0(i, 1)
                        D0(i)
                    if realv(i - 1):
                        P3(i - 1)
                    if realv(i - 2):
                        P4D4(i - 2)

            for h in range(NHD):
                ret_head(h)

            if STOP_AT <= 6:
                return
            areset()
            xn = sb("xn", [128, NR, D])
            h2T = sb("h2T", [128, NR, 16, 128], BF16)
            gam = sb("gam", [128, D])
            mark3 = ar['off']
            hb = [sb("hb%d" % i, [128, D], BF16) for i in range(2)]
            for ri in range(NR):
                dma('sp', xn[:, ri, :], xs[8 + ri], (), ['xn%d' % ri])
            wo = [sb("wo%d" % i, [128, 16, 512], BF16) for i in range(2)]
            for cb in range(4):
                b = cb % 2
                dma('pool', wo[b][:], w_out[:, cb * 512:(cb + 1) * 512].rearrange("(k p) c -> p k c", p=128),
                    (), ['wo%d' % b])
                for ri in range(NR):
                    pb = ri % 2
                    for k in range(16):
                        mm(ps[pb][:], mixT[:, k, ri * 128:(ri + 1) * 128], wo[b][:, k, :], k == 0, k == 15,
                           ['mixS', 'mixR', 'wo%d' % b], [PK[pb]])
                    tt(xn[:, ri, cb * 512:(cb + 1) * 512], xn[:, ri, cb * 512:(cb + 1) * 512], ps[pb][:], ALU.add,
                       ['xn%d' % ri, PK[pb]], ['xn%d' % ri])

            if STOP_AT <= 7:
                return
            dma('sp', gam[:], g3[1], (), ['gam'])
            norm_s1(xn[:, 0, :], 'xn0', gam[:], hb[0][:], 'hb0', 0)
            for ri in range(NR):
                b = ri % 2
                if ri + 1 < NR:
                    norm_s1(xn[:, ri + 1, :], 'xn%d' % (ri + 1), gam[:], hb[1 - b][:], 'hb%d' % (1 - b), 1 - b)
                norm_s2(h2T[:, ri], 'hT%d' % ri, hb[b][:], 'hb%d' % b)

            if STOP_AT <= 8:
                return
            G = 4
            T.barrier()
            ar['off'] = mark3
            wgu = [sb("wgu%d" % i, [128, 16, 2, 128], BF16) for i in range(2)]
            w2 = [sb("w2_0", [128, G, D], BF16)] * 2
            actT = [sb("actT0", [128, G, NTOK], BF16)] * 2
            sil = [sb("sil%d" % i, [128, 512]) for i in range(2)]
            h2keys = ['hT%d' % i for i in range(NR)]
            n = 0
            for gi in range(DFF // 128 // G):
                gb = gi % 2
                for fi in range(G):
                    fc = gi * G + fi
                    b = fc % 2
                    dma('pool', wgu[b][:, :, 0, :], w_f1[:, fc * 128:(fc + 1) * 128].rearrange("(k p) c -> p k c", p=128),
                        (), ['wgu%d' % b])
                    dma('pool', wgu[b][:, :, 1, :],
                        w_f1[:, DFF + fc * 128:DFF + (fc + 1) * 128].rearrange("(k p) c -> p k c", p=128),
                        (), ['wgu%d' % b])
                    if fi == G - 1:
                        dma('pool', w2[gb][:],
                            w_f2[gi * G * 128:(gi + 1) * G * 128, :].rearrange("(k p) c -> p k c", p=128), (), ['w2_0'])
                    for (c0, cn) in blocks:
                        pb = (n % 2) * 2
                        t0 = c0 // 128
                        ntl = cn // 128
                        for k in range(16):
                            mm(ps[pb][:, 0:cn], wgu[b][:, k, 0, :], h2T[:, t0:t0 + ntl, k, :], k == 0, k == 15,
                               ['wgu%d' % b] + h2keys[t0:t0 + ntl], [PK[pb]])
                        for k in range(16):
                            mm(ps[pb + 1][:, 0:cn], wgu[b][:, k, 1, :], h2T[:, t0:t0 + ntl, k, :], k == 0, k == 15,
                               ['wgu%d' % b] + h2keys[t0:t0 + ntl], [PK[pb + 1]])
                        sb_ = n % 2
                        act(sil[sb_][:, 0:cn], ps[pb][:, 0:cn], AF.Silu, [PK[pb]], ['sil%d' % sb_])
                        tt(actT[gb][:, fi, c0:c0 + cn], sil[sb_][:, 0:cn], ps[pb + 1][:, 0:cn], ALU.mult,
                           ['sil%d' % sb_, PK[pb + 1]], ['actT0'])
                        n += 1
                for ri in range(NR):
                    for cb in range(4):
                        bank = 4 + cb
                        for k in range(G):
                            mm(ps[bank][:], actT[gb][:, k, ri * 128:(ri + 1) * 128], w2[gb][:, k, cb * 512:(cb + 1) * 512],
                               k == 0, k == G - 1, ['actT0', 'w2_0'], [PK[bank]])
                        tt(xn[:, ri, cb * 512:(cb + 1) * 512], xn[:, ri, cb * 512:(cb + 1) * 512], ps[bank][:], ALU.add,
                           ['xn%d' % ri, PK[bank]], ['xn%d' % ri], eng='dve')

            if STOP_AT <= 9:
                return
            dma('sp', gam[:], g3[2], (), ['gam'])
            T.barrier()
            ar['off'] = mark3
            yo = [sb("yo%d" % i, [128, D]) for i in range(2)]
            sqs = sb("sqs", [128, D])
            for ri in range(NR):
                b = ri % 2
                act(sqs[:], xn[:, ri, :], AF.Square, ['xn%d' % ri], ['sqs', 'small8'], accum=small[:, 8:9])
                ts(small[:, 9:10], small[:, 8:9], 1.0 / D, EPS, ALU.mult, ALU.add, ['small8'], ['small9'])
                act(small[:, 9:10], small[:, 9:10], AF.Sqrt, ['small9'], ['small9'])
                T.op('dve', lambda e: e.reciprocal(out=small[:, 9:10], in_=small[:, 9:10]), ['small9'], ['small9'])
                stt(yo[b][:], xn[:, ri, :], small[:, 9:10], gam[:], ALU.mult, ALU.mult,
                    ['xn%d' % ri, 'small9', 'gam'], ['yo%d' % b])
                dma('sp', y_o[ri], yo[b][:], ['yo%d' % b], ['y_o%d' % ri])

        _phases()
        T.finish()
        with nc.Block() as block:
            @block.tensor
            def _(e):
                T.replay('pe', e)

            @block.vector
            def _(e):
                T.replay('dve', e)

            @block.scalar
            def _(e):
                T.replay('act', e)

            @block.gpsimd
            def _(e):
                T.replay('pool', e)

            @block.sync
            def _(e):
                T.replay('sp', e)
    return nc


def _host_consts(core):
    half = core % 2
    inv_freq = (10000.0 ** (-np.arange(64, dtype=np.float32) / np.float32(64))).astype(np.float32)
    pos = np.zeros((NT, 128), np.float32)
    for t in range(8):
        pos[t] = t * 128 + np.arange(128)
        pos[8 + t] = half * 1024 + t * 128 + np.arange(128)
    pos[16] = 16384 + (np.arange(128) % 8)
    ang = (pos[:, :, None].astype(np.float32) * inv_freq[None, None, :]).astype(np.float32)
    c = np.cos(ang.astype(np.float64)).astype(np.float32)
    s = np.sin(ang.astype(np.float64)).astype(np.float32)
    sc = np.float32(128 ** -0.5)
    lg = np.log(np.array(GAMMA, np.float64))
    idx = np.arange(128)
    maskT = np.zeros((2, 128, 8, 128), np.float32)
    qdec = np.zeros((2, 128, 8, 128), np.float32)
    kdec = np.zeros((2, 128, 8), np.float32)
    for h in range(8):
        dm = idx[None, :] - idx[:, None]
        maskT[0, :, h, :] = np.where(dm >= 0, np.exp(lg[h] * np.maximum(dm, 0)), 0.0)
        same = (idx[None, :] // 8) == (idx[:, None] // 8)
        maskT[1, :, h, :] = np.where((dm >= 0) & same, np.exp(lg[h] * np.maximum(dm, 0)), 0.0)
        qdec[0, :, h, :] = np.exp(lg[h] * (idx + 1.0))[None, :]
        qdec[1, :, h, :] = np.exp(lg[h] * ((idx % 8) + 1.0))[None, :]
        kdec[0, :, h] = np.exp(lg[h] * (127.0 - idx))
        kdec[1, :, h] = np.exp(lg[h] * (7.0 - (idx % 8)))
    bm01 = np.zeros((128, 16, 128), np.float32)
    rowm = np.zeros((128, 16), np.float32)
    for s_ in range(16):
        bm01[:, s_, s_ * 8:(s_ + 1) * 8] = 1.0
        rowm[s_ * 8:(s_ + 1) * 8, s_] = 1.0
    cqk = np.stack([c, (c * sc).astype(np.float32)], axis=2)
    sqk = np.stack([s, (s * sc).astype(np.float32)], axis=2)
    return dict(cqk=np.ascontiguousarray(cqk), sqk=np.ascontiguousarray(sqk),
                maskT=maskT, qdec=qdec, kdec=kdec, bm01=bm01, rowm=rowm,
                idn=np.eye(128, dtype=np.float32))


def _sm(a):
    return np.ascontiguousarray(a.reshape(32, 2, 64).transpose(1, 2, 0).reshape(128, 32))


_NC = None


def _prepare(x_prompt, x_sample, state_s5_re, state_s5_im, state_ret,
           norm_mix, w_in, s5_lambda_re, s5_lambda_im, s5_log_step, s5_b_re, s5_b_im,
           s5_c_re, s5_c_im, s5_d, s5_w_glu, s5_b_glu, ret_gn_w, w_out, norm_ffn,
           w_ffn_in, w_ffn_out, norm_final):
    global _NC
    f = lambda a: np.ascontiguousarray(np.asarray(a, dtype=np.float32))
    x_prompt, x_sample = f(x_prompt), f(x_sample)
    g3 = np.stack([np.broadcast_to(f(v).reshape(1, D), (128, D)) for v in (norm_mix[0], norm_ffn[0], norm_final)])
    gnw = np.ascontiguousarray(np.broadcast_to(f(ret_gn_w[0]).reshape(1, 1024), (128, 1024)))
    lam = np.stack([_sm(f(s5_lambda_re[0])), _sm(f(s5_lambda_im[0])),
                    _sm(np.broadcast_to(f(s5_log_step[0]).reshape(64, 1), (64, 64)))], axis=1)
    bre, bim = f(s5_b_re[0]), f(s5_b_im[0])
    cre, cim = f(s5_c_re[0]), f(s5_c_im[0])
    btp = np.zeros((128, 32, 2, 128), np.float32)
    ctp = np.zeros((128, 32, 2, 128), np.float32)
    for j in range(32):
        q = j % 4
        for hf in range(2):
            g = 2 * j + hf
            rows = slice(32 * q + 16 * hf, 32 * q + 16 * hf + 16)
            cols = slice(64 * hf, 64 * hf + 64)
            btp[rows, j, 0, cols] = bre[g].T
            btp[rows, j, 1, cols] = bim[g].T
            ctp[cols, j, 0, rows] = cre[g].T
            ctp[cols, j, 1, rows] = cim[g].T
    dsm = np.ascontiguousarray(f(s5_d[0]).reshape(8, 128).T)
    bglu = np.ascontiguousarray(f(s5_b_glu[0]).reshape(8, 128).T)
    shared = dict(w_in=f(w_in[0]), w_out=f(w_out[0]), w_f1=f(w_ffn_in[0]), w_f2=f(w_ffn_out[0]),
                  w_glu=f(s5_w_glu[0]), g3=np.ascontiguousarray(g3), gnw=gnw, lam=np.ascontiguousarray(lam),
                  btp=btp, ctp=ctp, dsm=dsm, bglu=bglu)
    sre, sim, sret = f(state_s5_re[0]), f(state_s5_im[0]), f(state_ret[0])
    in_maps = []
    for c in range(8):
        seq, half = c // 2, c % 2
        xs = np.zeros((NT, 128, D), np.float32)
        if half == 1:
            xs[0:8] = x_prompt[seq, 0:1024].reshape(8, 128, D)
        xs[8:16] = x_prompt[seq, half * 1024:(half + 1) * 1024].reshape(8, 128, D)
        xs[16] = x_sample[c * 16:(c + 1) * 16].reshape(128, D)
        s5i = np.zeros((128, 16, 2, 32), np.float32)
        for s_ in range(16):
            s5i[:, s_, 0, :] = _sm(sre[c * 16 + s_])
            s5i[:, s_, 1, :] = _sm(sim[c * 16 + s_])
        m = dict(shared)
        m.update(_host_consts(c))
        m.update(xs=xs, s5i=s5i, r0=np.ascontiguousarray(sret[c * 16:(c + 1) * 16]))
        in_maps.append(m)
    return in_maps


def kernel(**inputs):
    global _NC
    in_maps = _prepare(**inputs)
    if _NC is None:
        _NC = build()
    res = run_bass_kernel_spmd(_NC, in_maps, core_ids=list(range(8)))
    return _assemble(res.results)


def _assemble(rs):
    y_prompt = np.zeros((4, 2048, D), np.float32)
    y_sample = np.zeros((128, 8, D), np.float32)
    p_re = np.zeros((1, 4, 64, 64), np.float32); p_im = np.zeros((1, 4, 64, 64), np.float32)
    p_ret = np.zeros((1, 4, 8, 128, 128), np.float32)
    s_re = np.zeros((1, 128, 64, 64), np.float32); s_im = np.zeros((1, 128, 64, 64), np.float32)
    s_ret = np.zeros((1, 128, 8, 128, 128), np.float32)

    def unsm(a):
        return a.reshape(2, 64, 32).transpose(2, 0, 1).reshape(64, 64)

    for c in range(8):
        seq, half = c // 2, c % 2
        r = rs[c]
        y = np.asarray(r["y"], np.float32)
        y_prompt[seq, half * 1024:(half + 1) * 1024] = y[0:8].reshape(1024, D)
        y_sample[c * 16:(c + 1) * 16] = y[8].reshape(16, 8, D)
        if half == 1:
            sp = np.asarray(r["s5p"], np.float32)
            p_re[0, seq] = unsm(sp[:, 0, :]); p_im[0, seq] = unsm(sp[:, 1, :])
            p_ret[0, seq] = np.asarray(r["retp"], np.float32)
        ss = np.asarray(r["s5s"], np.float32)
        for s_ in range(16):
            s_re[0, c * 16 + s_] = unsm(ss[:, s_, 0, :]); s_im[0, c * 16 + s_] = unsm(ss[:, s_, 1, :])
        s_ret[0, c * 16:(c + 1) * 16] = np.asarray(r["rets"], np.float32)
    return (y_prompt, y_sample, p_re, p_im, p_ret, s_re, s_im, s_ret)
```

```python
import math
from contextlib import ExitStack
import numpy as np
import concourse.bass as bass
import concourse.mybir as mybir
from concourse.bass_utils import run_bass_kernel_spmd

F32 = mybir.dt.float32
BF16 = mybir.dt.bfloat16
AF = mybir.ActivationFunctionType
ALU = mybir.AluOpType

D = 2048
NT = 17
NR = 9
NTOK = NR * 128
DFF = 5632
EPS = 1e-6
GAMMA = [1.0 - 2.0 ** (-5.0 - h) for h in range(8)]


class TR:
    def __init__(self, nc, es):
        self.nc = nc
        self.es = es
        self.streams = {k: [] for k in ('pe', 'dve', 'act', 'pool', 'sp')}
        self.sems = {n: es.enter_context(nc.semaphore(n)) for n in ('pe', 'dve', 'act', 'pool')}
        self.cnt = {n: 0 for n in self.sems}
        self.waited = {}
        self.lw = {}
        self.rd = {}

    @staticmethod
    def _mul(s):
        return 16 if s.startswith('d_') else 1

    def op(self, stream, fn, reads=(), writes=(), dma=False):
        if dma:
            semn = 'd_' + writes[0]
            if semn not in self.sems:
                self.sems[semn] = self.es.enter_context(self.nc.semaphore(semn))
                self.cnt[semn] = 0
        else:
            semn = stream
        deps = {}

        def add(ev, raw):
            if ev is None:
                return
            s, n = ev
            if s == stream and s == 'pe':
                return
            deps[s] = max(deps.get(s, 0), n)

        if dma and self.cnt[semn]:
            deps[semn] = self.cnt[semn]
        for k in reads:
            add(self.lw.get(k), True)
            if k.startswith('ps'):
                for s, n in self.rd.get(k, {}).items():
                    if s != semn:
                        add((s, n), False)
        for k in writes:
            add(self.lw.get(k), False)
            for s, n in self.rd.get(k, {}).items():
                add((s, n), False)
        for s, n in deps.items():
            val = n * self._mul(s)
            if self.waited.get((stream, s), 0) >= val:
                continue
            self.waited[(stream, s)] = val
            self.streams[stream].append(('w', s, val))
        self.cnt[semn] += 1
        ev = (semn, self.cnt[semn])
        self.streams[stream].append(('o', fn, semn))
        for k in writes:
            self.lw[k] = ev
            self.rd[k] = {}
        for k in reads:
            d = self.rd.setdefault(k, {})
            d[semn] = max(d.get(semn, 0), ev[1])

    def barrier(self):
        for st in self.streams:
            for s_, c in self.cnt.items():
                if not c:
                    continue
                val = c * self._mul(s_)
                if self.waited.get((st, s_), 0) >= val:
                    continue
                self.waited[(st, s_)] = val
                self.streams[st].append(('w', s_, val))

    def finish(self):
        for s_, c in self.cnt.items():
            if c:
                val = c * self._mul(s_)
                if self.waited.get(('sp', s_), 0) >= val:
                    continue
                self.streams['sp'].append(('w', s_, val))

    def replay(self, stream, eng):
        for it in self.streams[stream]:
            if it[0] == 'w':
                eng.wait_ge(self.sems[it[1]], it[2])
            else:
                ins = it[1](eng)
                ins.then_inc(self.sems[it[2]], self._mul(it[2]))


import os
STOP_AT = int(os.environ.get('KSTOP', '99'))


def build():
    nc = bass.Bass("TRN2", target_bir_lowering=False)
    es = ExitStack()
    with es:
        def din(name, shape, dt=F32):
            return nc.dram_tensor(name, list(shape), dt, kind="ExternalInput").ap()

        def dout(name, shape):
            return nc.dram_tensor(name, list(shape), F32, kind="ExternalOutput").ap()

        xs = din("xs", [NT, 128, D])
        w_in = din("w_in", [D, 5120])
        w_out = din("w_out", [D, D])
        w_f1 = din("w_f1", [D, 2 * DFF])
        w_f2 = din("w_f2", [DFF, D])
        w_glu = din("w_glu", [1024, 1024])
        g3 = din("g3", [3, 128, D])
        gnw = din("gnw", [128, 1024])
        lam = din("lam", [128, 3, 32])
        btp = din("btp", [128, 32, 2, 128])
        ctp = din("ctp", [128, 32, 2, 128])
        dsm = din("dsm", [128, 8])
        bglu = din("bglu", [128, 8])
        s5i = din("s5i", [128, 16, 2, 32])
        r0 = din("r0", [16, 8, 128, 128])
        cqk = din("cqk", [NT, 128, 2, 64])
        sqk = din("sqk", [NT, 128, 2, 64])
        maskT = din("maskT", [2, 128, 8, 128])
        qdec = din("qdec", [2, 128, 8, 128])
        kdec = din("kdec", [2, 128, 8])
        bm01 = din("bm01", [128, 16, 128])
        rowm = din("rowm", [128, 16])
        idn = din("idn", [128, 128])

        y_o = dout("y", [NR, 128, D])
        s5p_o = dout("s5p", [128, 2, 32])
        s5s_o = dout("s5s", [128, 16, 2, 32])
        retp_o = dout("retp", [8, 128, 128])
        rets_o = dout("rets", [16, 8, 128, 128])

        T = TR(nc, es)
        psb = lambda name, shape, dt=F32: es.enter_context(nc.sbuf_tensor(name, list(shape), dt))
        ARN = 83 * 1024
        arena = psb("arena", [128, ARN], BF16)
        ar = {'off': 0}

        def areset():
            T.barrier()
            ar['off'] = 0

        def sb(name, shape, dt=F32):
            n = 1
            for d_ in shape[1:]:
                n *= d_
            w = n * (2 if dt == F32 else 1)
            w = (w + 15) // 16 * 16
            o = ar['off']
            assert o + w <= ARN, (name, o, w)
            ar['off'] = o + w
            v = arena[:, o:o + w]
            if dt == F32:
                v = v.bitcast(F32)
            v = v[:, 0:n]
            if len(shape) == 2:
                return v
            names = " ".join("d%d" % i for i in range(1, len(shape)))
            kw = {"d%d" % i: shape[i] for i in range(1, len(shape))}
            return v.rearrange("p (%s) -> p %s" % (names, names), **kw)
        ps = [es.enter_context(nc.psum_tensor("ps%d" % i, [128, 512], F32)) for i in range(8)]
        PK = ["ps%d" % i for i in range(8)]

        def dma(stream, out, in_, reads, writes):
            T.op(stream, lambda e: e.dma_start(out=out, in_=in_), reads, writes, dma=True)

        def act(out, in_, func, reads, writes, bias=None, scale=None, accum=None):
            kw = {}
            if bias is not None:
                kw['bias'] = bias
            if scale is not None:
                kw['scale'] = scale
            if accum is not None:
                kw['accum_out'] = accum
            T.op('act', lambda e: e.activation(out=out, in_=in_, func=func, **kw), reads, writes)

        def tt(out, a, b, op, reads, writes, eng='dve'):
            T.op(eng, lambda e: e.tensor_tensor(out=out, in0=a, in1=b, op=op), reads, writes)

        def ts(out, a, s1, s2, op0, op1, reads, writes, eng='dve'):
            if s2 is None:
                T.op(eng, lambda e: e.tensor_scalar(out=out, in0=a, scalar1=s1, scalar2=None, op0=op0), reads, writes)
            else:
                T.op(eng, lambda e: e.tensor_scalar(out=out, in0=a, scalar1=s1, scalar2=s2, op0=op0, op1=op1), reads, writes)

        def stt(out, a, s, b, op0, op1, reads, writes, eng='dve'):
            T.op(eng, lambda e: e.scalar_tensor_tensor(out=out, in0=a, scalar=s, in1=b, op0=op0, op1=op1), reads, writes)

        def cp(out, in_, reads, writes, eng='dve'):
            if eng == 'act':
                T.op(eng, lambda e: e.activation(out=out, in_=in_, func=AF.Copy), reads, writes)
            else:
                T.op(eng, lambda e: e.tensor_copy(out=out, in_=in_), reads, writes)

        def mm(out, lhsT, rhs, start, stop, reads, writes):
            T.op('pe', lambda e: e.matmul(out, lhsT, rhs, start=start, stop=stop), reads, writes)

        def tp(out, in_, ident, reads, writes):
            T.op('pe', lambda e: e.transpose(out, in_, ident), reads, writes)

        def memset(ap, v, writes, eng='dve'):
            T.op(eng, lambda e: e.memset(ap, v), (), writes)

        def _phases():
            ident_f = psb("ident_f", [128, 128])
            ident = psb("ident", [128, 128], BF16)
            dma('sp', ident_f[:], idn, (), ['ident_f'])
            cp(ident[:], ident_f[:], ['ident_f'], ['ident'])
            small = psb("small", [128, 64])
            lam_t = psb("lam_t", [128, 3, 32])
            dma('sp', lam_t[:], lam, (), ['lam'])
            dsm_t = psb("dsm_t", [128, 8])
            bglu_t = psb("bglu_t", [128, 8])
            dma('sp', dsm_t[:], dsm, (), ['dsm'])
            dma('sp', bglu_t[:], bglu, (), ['bglu'])
            kdec_t = psb("kdec_t", [128, 2, 8])
            for i in range(2):
                dma('sp', kdec_t[:, i, :], kdec[i], (), ['kdec'])
            rowm_t = psb("rowm_t", [128, 16])
            dma('sp', rowm_t[:], rowm, (), ['rowm'])

            cf = psb("cf", [128, 16, 32])
            DT, A_, TH, EA, U0, F1, SN, CS, LR, LI, KR, KI, T0, T1, DEN, LIN = [cf[:, i, :] for i in range(16)]
            cf2 = psb("cf2", [128, 4, 32])
            IKR, IKI, T2, T3 = [cf2[:, i, :] for i in range(4)]
            C = ['cf']
            act(DT, lam_t[:, 2, :], AF.Exp, ['lam'], C)
            tt(A_, lam_t[:, 0, :], DT, ALU.mult, ['lam'] + C, C)
            tt(TH, lam_t[:, 1, :], DT, ALU.mult, ['lam'] + C, C)
            act(EA, A_, AF.Exp, C, C)
            ts(U0, TH, 1.0 / 16.0, None, ALU.mult, None, C, C)
            act(SN, U0, AF.Sin, C, C)
            ts(F1, U0, math.pi / 2, None, ALU.add, None, C, C)
            act(CS, F1, AF.Sin, C, C)
            for _ in range(4):
                tt(T0, SN, CS, ALU.mult, C, C)
                tt(T1, CS, CS, ALU.mult, C, C)
                tt(F1, SN, SN, ALU.mult, C, C)
                ts(SN, T0, 2.0, None, ALU.mult, None, C, C)
                tt(CS, T1, F1, ALU.subtract, C, C)
            tt(LR, EA, CS, ALU.mult, C, C)
            tt(LI, EA, SN, ALU.mult, C, C)
            ts(LIN, LI, -1.0, None, ALU.mult, None, C, C)
            ts(T0, LR, -1.0, None, ALU.add, None, C, C)
            tt(T1, lam_t[:, 0, :], lam_t[:, 0, :], ALU.mult, ['lam'] + C, C)
            tt(DEN, lam_t[:, 1, :], lam_t[:, 1, :], ALU.mult, ['lam'] + C, C)
            tt(DEN, DEN, T1, ALU.add, C, C)
            T.op('dve', lambda e: e.reciprocal(out=DEN, in_=DEN), C, C)
            tt(KR, T0, lam_t[:, 0, :], ALU.mult, ['lam'] + C, C)
            tt(T1, LI, lam_t[:, 1, :], ALU.mult, ['lam'] + C, C)
            tt(KR, KR, T1, ALU.add, C, C)
            tt(KR, KR, DEN, ALU.mult, C, C)
            tt(KI, LI, lam_t[:, 0, :], ALU.mult, ['lam'] + C, C)
            tt(T1, T0, lam_t[:, 1, :], ALU.mult, ['lam'] + C, C)
            tt(KI, KI, T1, ALU.subtract, C, C)
            tt(KI, KI, DEN, ALU.mult, C, C)
            C2 = ['cf2']
            tt(T2, KR, KR, ALU.mult, C, C2)
            tt(T3, KI, KI, ALU.mult, C, C2)
            tt(T2, T2, T3, ALU.add, C2, C2)
            T.op('dve', lambda e: e.reciprocal(out=T2, in_=T2), C2, C2)
            tt(IKR, KR, T2, ALU.mult, C + C2, C2)
            tt(IKI, KI, T2, ALU.mult, C + C2, C2)
            ts(IKI, IKI, -1.0, None, ALU.mult, None, C2, C2)
            LL = psb("LL", [128, 2, 2, 32])
            cp(LL[:, 0, 0, :], LR, C, ['LL'])
            cp(LL[:, 0, 1, :], LR, C, ['LL'])
            cp(LL[:, 1, 0, :], LIN, C, ['LL'])
            cp(LL[:, 1, 1, :], LI, C, ['LL'])

            def cmul_bc(out_re, out_im, a_re, a_im, cr, ci, tmp, rk, wk, neg_im=False):
                tt(out_re, a_re, cr, ALU.mult, rk, wk)
                tt(tmp, a_im, ci, ALU.mult, rk, wk)
                tt(out_re, out_re, tmp, ALU.subtract, wk, wk)
                tt(out_im, a_re, ci, ALU.mult, rk, wk)
                tt(tmp, a_im, cr, ALU.mult, rk, wk)
                tt(out_im, out_im, tmp, ALU.add, wk, wk)
                if neg_im:
                    ts(out_im, out_im, -1.0, None, ALU.mult, None, wk, wk)

            mixT = psb("mixT", [128, 16, NTOK], BF16)
            hT_d = nc.dram_tensor("hT_d", [NT, 128, 16, 128], BF16).ap()

            if STOP_AT <= 1:
                return
            def norm_s1(x_t, xk, gam_ap, hb, hbk, par):
                c0 = 2 * par
                sk0, sk1 = 'small%da' % par, 'small%db' % par
                act(hb, x_t, AF.Square, [xk], [hbk, sk0], accum=small[:, c0:c0 + 1])
                ts(small[:, c0 + 1:c0 + 2], small[:, c0:c0 + 1], 1.0 / D, EPS, ALU.mult, ALU.add, [sk0], [sk1])
                act(small[:, c0 + 1:c0 + 2], small[:, c0 + 1:c0 + 2], AF.Sqrt, [sk1], [sk1])
                T.op('dve', lambda e: e.reciprocal(out=small[:, c0 + 1:c0 + 2], in_=small[:, c0 + 1:c0 + 2]), [sk1], [sk1])
                stt(hb, x_t, small[:, c0 + 1:c0 + 2], gam_ap, ALU.mult, ALU.mult, [xk, sk1, 'gam'], [hbk])

            def norm_s2(outT, outk, hb, hbk):
                for half in range(2):
                    pt = ps[6 + half]
                    ptb = pt[:].bitcast(BF16)
                    for kk in range(8):
                        k = half * 8 + kk
                        tp(ptb[:, kk * 128:(kk + 1) * 128], hb[:, k * 128:(k + 1) * 128], ident[:],
                           [hbk, 'ident'], [PK[6 + half]])
                    cp(outT[:, half * 8:(half + 1) * 8, :],
                       ptb[:, 0:1024].rearrange("p (k t) -> p k t", k=8), [PK[6 + half]], [outk],
                       eng='act' if half else 'dve')

            gam = sb("gam", [128, D])
            dma('sp', gam[:], g3[0], (), ['gam'])
            xa = [sb("xa%d" % i, [128, D]) for i in range(2)]
            hb = [sb("hb%d" % i, [128, D], BF16) for i in range(2)]
            hTt = [sb("hTt%d" % i, [128, 16, 128], BF16) for i in range(2)]
            def a_ld(t):
                dma('sp', xa[t % 2][:], xs[t], (), ['xa%d' % (t % 2)])

            def a_s1(t):
                b = t % 2
                norm_s1(xa[b][:], 'xa%d' % b, gam[:], hb[b][:], 'hb%d' % b, b)

            a_ld(0)
            a_ld(1)
            a_s1(0)
            for t in range(NT):
                b = t % 2
                if t + 1 < NT:
                    a_s1(t + 1)
                if t + 2 < NT:
                    a_ld(t + 2)
                norm_s2(hTt[b], 'hTt%d' % b, hb[b][:], 'hb%d' % b)
                dma('sp', hT_d[t], hTt[b][:], ['hTt%d' % b], ['hT_d%d' % b])

            if STOP_AT <= 2:
                return
            areset()
            uT = sb("uT", [128, 8, NT * 128], BF16)
            mark1 = ar['off']
            hblk = [sb("hblk%d" % i, [128, 4, 16, 128], BF16) for i in range(2)]
            wblk = [sb("wblk%d" % i, [128, 16, 128], BF16) for i in range(8)]
            for ct in range(8):
                dma('pool', wblk[ct][:], w_in[:, ct * 128:(ct + 1) * 128].rearrange("(k p) c -> p k c", p=128),
                    (), ['wblk%d' % ct])
            n = 0
            for blk in range(5):
                t0 = blk * 4
                nt = min(4, NT - t0)
                hbk = 'hblk%d' % (blk % 2)
                dma('sp', hblk[blk % 2][:, 0:nt], hT_d[t0:t0 + nt].rearrange("t p k c -> p t k c"), ['hT_d0', 'hT_d1'], [hbk])
                for ct in range(8):
                    pb = n % 2
                    n += 1
                    for k in range(16):
                        mm(ps[pb][:, 0:nt * 128], wblk[ct][:, k, :],
                           hblk[blk % 2][:, 0:nt, k, :], k == 0, k == 15, ['wblk%d' % ct, hbk], [PK[pb]])
                    cp(uT[:, ct, t0 * 128:(t0 + nt) * 128], ps[pb][:, 0:nt * 128], [PK[pb]], ['uT'],
                       eng='act' if n % 2 else 'dve')

            if STOP_AT <= 3:
                return
            T.barrier()
            ar['off'] = mark1
            Bv = sb("Bv", [128, 32, 2, 128], BF16)
            dma('pool', Bv[:], btp, (), ['Bv'])
            Cv = sb("Cv", [128, 32, 2, 128], BF16)
            mark2 = ar['off']
            ctf = sb("ctf", [128, 8, 2, 128])
            cto = sb("cto", [128, 8, 3, 128])
            for g in range(4):
                dma('sp', ctf[:], ctp[:, g * 8:(g + 1) * 8], (), ['ctf'])
                krb = KR[:, g * 8:(g + 1) * 8].unsqueeze(2).broadcast_to([128, 8, 128])
                kib = KI[:, g * 8:(g + 1) * 8].unsqueeze(2).broadcast_to([128, 8, 128])
                cmul_bc(cto[:, :, 0, :], cto[:, :, 1, :], ctf[:, :, 0, :], ctf[:, :, 1, :], krb, kib,
                        cto[:, :, 2, :], ['ctf'] + C, ['cto'], neg_im=True)
                cp(Cv[:, g * 8:(g + 1) * 8, 0, :], cto[:, :, 0, :], ['cto'], ['Cv'])
                cp(Cv[:, g * 8:(g + 1) * 8, 1, :], cto[:, :, 1, :], ['cto'], ['Cv'])
            T.barrier()
            ar['off'] = mark2
            TC = 64
            Ec = sb("Ec", [128, 32, TC])
            Es = sb("Es", [128, 32, TC])
            Rt = sb("Rt", [128, 32, TC])
            wzb = [sb("wz%d" % i, [128, 2, 32, TC]) for i in range(2)]
            Xb = sb("Xb", [128, 2, 32, TC], BF16)
            rt1 = sb("rt1", [128, 32, TC])
            etmp = rt1
            rt2 = sb("rt2", [128, 32, TC])
            carry = sb("carry", [128, 8, 2, 32])
            ctmp = sb("ctmp", [128, 8, 2, 32])
            ctm2 = sb("ctm2", [128, 32, 8])
            ysb = sb("ysb", [128, 8, TC])
            yt1 = sb("yt1", [128, 8, TC])
            fin = sb("fin", [128, 16, 2, 32])
            ftmp = sb("ftmp", [128, 16, 32])
            s5i_t = sb("s5i_t", [128, 16, 2, 32])
            EK = ['E']
            cp(Ec[:, :, 0], CS, C, EK)
            cp(Es[:, :, 0], SN, C, EK)
            ln = 1
            while ln < TC:
                cmul_bc(Ec[:, :, ln:2 * ln], Es[:, :, ln:2 * ln], Ec[:, :, 0:ln], Es[:, :, 0:ln],
                        Ec[:, :, ln - 1:ln].broadcast_to([128, 32, ln]), Es[:, :, ln - 1:ln].broadcast_to([128, 32, ln]),
                        etmp[:, :, 0:ln], EK, EK)
                ln *= 2
            cp(Rt[:], EA.unsqueeze(2).broadcast_to([128, 32, TC]), C, ['Rt'])
            memset(Rt[:, :, 0:1], 0.0, ['Rt'])
            memset(carry[:], 0.0, ['carry'])

            def s5_parts(t, hf, nseq, real_idx, par):
                steps = TC // nseq
                tok0 = t * 128 + hf * TC
                wz = wzb[par]
                WK = 'wz%d' % par

                def A_pe():
                    for g in range(4):
                        bre = ps[(g % 2) * 2]
                        bim = ps[(g % 2) * 2 + 1]
                        kre = PK[(g % 2) * 2]
                        kim = PK[(g % 2) * 2 + 1]
                        for jj in range(8):
                            j = g * 8 + jj
                            mm(bre[:, jj * TC:(jj + 1) * TC], Bv[:, j, 0, :], uT[:, j // 4, tok0:tok0 + TC], True, True,
                               ['Bv', 'uT'], [kre])
                        for jj in range(8):
                            j = g * 8 + jj
                            mm(bim[:, jj * TC:(jj + 1) * TC], Bv[:, j, 1, :], uT[:, j // 4, tok0:tok0 + TC], True, True,
                               ['Bv', 'uT'], [kim])
                        act(wz[:, 0, g * 8:(g + 1) * 8, :], bre[:, :].rearrange("p (j t) -> p j t", j=8), AF.Copy,
                            [kre], [WK])
                        act(wz[:, 1, g * 8:(g + 1) * 8, :], bim[:, :].rearrange("p (j t) -> p j t", j=8), AF.Copy,
                            [kim], [WK])

                def FWD():
                    tt(rt1[:], wz[:, 1], Es[:], ALU.mult, [WK, 'E'], ['rt1'])
                    tt(rt2[:], wz[:, 0], Es[:], ALU.mult, [WK, 'E'], ['rt2'])
                    tt(wz[:, 0], wz[:, 0], Ec[:], ALU.mult, [WK, 'E'], [WK])
                    tt(wz[:, 0], wz[:, 0], rt1[:], ALU.add, [WK, 'rt1'], [WK])
                    tt(wz[:, 1], wz[:, 1], Ec[:], ALU.mult, [WK, 'E'], [WK])
                    tt(wz[:, 1], wz[:, 1], rt2[:], ALU.subtract, [WK, 'rt2'], [WK])

                def Bp():
                    tt(ctmp[:, 0:nseq], carry[:, 0:nseq],
                       EA.unsqueeze(1).unsqueeze(1).broadcast_to([128, nseq, 2, 32]), ALU.mult, ['carry'] + C, ['ctmp'])
                    wfirst = wz[:, :, :, 0::steps]
                    tt(wfirst, wfirst, ctmp[:, 0:nseq].rearrange("p s r j -> p r j s"), ALU.add, [WK, 'ctmp'], [WK])
                    for r in range(2):
                        zf = wz[:, r].rearrange("p j t -> p (j t)")
                        T.op('dve', lambda e, zf=zf: e.tensor_tensor_scan(
                            out=zf, data0=Rt[:].rearrange("p j t -> p (j t)"), data1=zf, initial=0.0,
                            op0=ALU.mult, op1=ALU.add), [WK, 'Rt'], [WK])
                    cv = carry[:, 0:nseq].rearrange("p s r j -> p r j s")
                    cmul_bc(cv[:, 0], cv[:, 1], wz[:, 0, :, steps - 1::steps], wz[:, 1, :, steps - 1::steps],
                            Ec[:, :, steps - 1:steps].broadcast_to([128, 32, nseq]),
                            Es[:, :, steps - 1:steps].broadcast_to([128, 32, nseq]),
                            ctm2[:, :, 0:nseq], [WK, 'E'], ['carry'])
                    if real_idx is None:
                        return
                    tt(rt1[:], wz[:, 0], Ec[:], ALU.mult, [WK, 'E'], ['rt1'])
                    tt(rt2[:], wz[:, 1], Es[:], ALU.mult, [WK, 'E'], ['rt2'])
                    tt(Xb[:, 0], rt1[:], rt2[:], ALU.subtract, ['rt1', 'rt2'], ['Xb'])
                    tt(rt1[:], wz[:, 0], Es[:], ALU.mult, [WK, 'E'], ['rt1'])
                    tt(rt2[:], wz[:, 1], Ec[:], ALU.mult, [WK, 'E'], ['rt2'])
                    tt(Xb[:, 1], rt1[:], rt2[:], ALU.add, ['rt1', 'rt2'], ['Xb'])
                    for ct in range(8):
                        n = 0
                        for q in range(4):
                            j = ct * 4 + q
                            for r in range(2):
                                mm(ps[4][:, ct * TC:(ct + 1) * TC], Cv[:, j, r, :], Xb[:, r, j, :],
                                   n == 0, n == 7, ['Cv', 'Xb'], [PK[4]])
                                n += 1

                def Cp():
                    if real_idx is None:
                        return
                    tt(yt1[:], uT[:, :, tok0:tok0 + TC], dsm_t[:].unsqueeze(2).broadcast_to([128, 8, TC]), ALU.mult,
                       ['uT', 'dsm'], ['yt1'])
                    tt(ysb[:], yt1[:], ps[4][:, :].rearrange("p (c t) -> p c t", c=8), ALU.add, ['yt1', PK[4]], ['ysb'])
                    tt(yt1[:], ysb[:], ysb[:], ALU.mult, ['ysb'], ['yt1'])
                    ts(yt1[:], yt1[:], 0.044715, 1.0, ALU.mult, ALU.add, ['yt1'], ['yt1'])
                    tt(yt1[:], yt1[:], ysb[:], ALU.mult, ['yt1', 'ysb'], ['yt1'])
                    act(yt1[:], yt1[:], AF.Sigmoid, ['yt1'], ['yt1'], scale=2.0 * math.sqrt(2.0 / math.pi))
                    c0 = real_idx * 128 + hf * TC
                    tt(mixT[:, 0:8, c0:c0 + TC], ysb[:], yt1[:], ALU.mult, ['ysb', 'yt1'], ['mixS'])

                return A_pe, FWD, Bp, Cp

            halves = [s5_parts(t, hf, 1, (t - 8) if t >= 8 else None, (2 * t + hf) % 2)
                      for t in range(16) for hf in range(2)]
            halves[0][0]()
            halves[0][1]()
            for n_ in range(len(halves)):
                if n_ + 1 < len(halves):
                    halves[n_ + 1][0]()
                halves[n_][2]()
                if n_ + 1 < len(halves):
                    halves[n_ + 1][1]()
                halves[n_][3]()

            def s5_half(t, hf, nseq, real_idx):
                A_pe, FWD, Bp, Cp = s5_parts(t, hf, nseq, real_idx, hf)
                A_pe(); FWD(); Bp(); Cp()

            cmul_bc(fin[:, 0:1, 0, :], fin[:, 0:1, 1, :], carry[:, 0:1, 0, :], carry[:, 0:1, 1, :],
                    KR.unsqueeze(1), KI.unsqueeze(1), ftmp[:, 0:1, :], ['carry'] + C, ['fin'])
            dma('sp', s5p_o, fin[:, 0], ['fin'], ['s5p_o'])
            dma('sp', s5i_t[:], s5i, (), ['s5i'])
            for tb in (Ec, Es):
                cp(tb[:, :, 8:TC].rearrange("p j (s t) -> p j s t", t=8),
                   tb[:, :, 0:8].unsqueeze(2).broadcast_to([128, 32, TC // 8 - 1, 8]), EK, EK)
            memset(Rt[:, :, 8::8], 0.0, ['Rt'])
            for hf in range(2):
                sl = slice(hf * 8, hf * 8 + 8)
                cmul_bc(carry[:, :, 0, :], carry[:, :, 1, :], s5i_t[:, sl, 0, :], s5i_t[:, sl, 1, :],
                        IKR.unsqueeze(1).broadcast_to([128, 8, 32]), IKI.unsqueeze(1).broadcast_to([128, 8, 32]),
                        ftmp[:, 0:8, :], ['s5i'] + C2, ['carry'])
                s5_half(16, hf, 8, 8)
                cmul_bc(fin[:, sl, 0, :], fin[:, sl, 1, :], carry[:, :, 0, :], carry[:, :, 1, :],
                        KR.unsqueeze(1).broadcast_to([128, 8, 32]), KI.unsqueeze(1).broadcast_to([128, 8, 32]),
                        ftmp[:, 0:8, :], ['carry'] + C, ['fin'])
            dma('sp', s5s_o, fin[:], ['fin'], ['s5s_o'])

            if STOP_AT <= 4:
                return
            areset()
            wg = sb("wg", [128, 8, 1024], BF16)
            dma('pool', wg[:], w_glu.rearrange("(k p) c -> p k c", p=128), (), ['wg'])
            glu_o = sb("glu_o", [128, 8, NTOK], BF16)
            sg_t = [sb("sg_t%d" % i, [128, 512]) for i in range(2)]
            blocks = [(0, 512), (512, 512), (1024, 128)]
            n = 0
            for m in range(8):
                for (c0, cn) in blocks:
                    pb = n % 2
                    for k in range(8):
                        mm(ps[pb][:, 0:cn], wg[:, k, m * 128:(m + 1) * 128], mixT[:, k, c0:c0 + cn], k == 0, k == 7,
                           ['wg', 'mixS'], [PK[pb]])
                    act(sg_t[pb][:, 0:cn], ps[pb][:, 0:cn], AF.Sigmoid, [PK[pb], 'bglu'], ['sg_t%d' % pb],
                        bias=bglu_t[:, m:m + 1])
                    tt(glu_o[:, m, c0:c0 + cn], mixT[:, m, c0:c0 + cn], sg_t[pb][:, 0:cn], ALU.mult,
                       ['mixS', 'sg_t%d' % pb], ['glu_o'])
                    n += 1
            cp(mixT[:, 0:8, :], glu_o[:], ['glu_o', 'mixS'], ['mixS'], eng='pool')

            if STOP_AT <= 5:
                return
            areset()
            hTc = [sb("hTc%d" % i, [128, 16, 128], BF16) for i in range(2)]
            gnw_t = sb("gnw_t", [128, 1024])
            dma('sp', gnw_t[:], gnw, (), ['gnw'])
            cqk_t = sb("cqk_t", [128, NT, 2, 64])
            sqk_t = sb("sqk_t", [128, NT, 2, 64])
            dma('sp', cqk_t[:], cqk.rearrange("t p a f -> p t a f"), (), ['rot'])
            dma('sp', sqk_t[:], sqk.rearrange("t p a f -> p t a f"), (), ['rot2'])
            eps_t = sb("eps_t", [128, 1])
            memset(eps_t[:], EPS, ['eps_t'])
            maskT_t = sb("maskT_t", [128, 2, 8, 128])
            qdec_t = sb("qdec_t", [128, 2, 8, 128])
            for i in range(2):
                dma('sp', maskT_t[:, i], maskT[i], (), ['maskT'])
                dma('sp', qdec_t[:, i], qdec[i], (), ['qdec'])
            bm01_t = sb("bm01_t", [128, 16, 128])
            dma('sp', bm01_t[:], bm01, (), ['bm01'])
            wh = [sb("wh%d" % i, [128, 16, 512], BF16) for i in range(2)]
            R = sb("R", [128, 128])
            Rb = [sb("Rb%d" % i, [128, 128], BF16) for i in range(2)]

            def two(name, shape, dt=F32):
                return [sb("%s%d" % (name, i), shape, dt) for i in range(2)]
            qk = two("qk", [128, 2, 128], BF16)
            rA = two("rA", [128, 2, 2, 64]); rB = two("rB", [128, 2, 2, 64])
            qkT = two("qkT", [128, 256], BF16)
            vsb = two("vsb", [128, 128], BF16); sgate = two("sgate", [128, 128])
            kd = two("kd", [128, 128], BF16)
            qT = two("qT", [128, 128], BF16); qdT = two("qdT", [128, 128], BF16); kT = two("kT", [128, 128], BF16)
            sT = two("sT", [128, 128], BF16)
            rt = two("rt", [128, 4, 64])
            rtq = two("rtq", [128, 4, 64])
            osb = two("osb", [128, 128]); osq = two("osq", [128, 128])
            ro = two("ro", [128, 128], BF16)
            st = two("st", [128, 8])
            qm = sb("qm", [128, 16, 128], BF16)
            km = sb("km", [128, 16, 128], BF16)
            r0f = sb("r0f", [128, 16, 128])
            r0b = sb("r0b", [128, 16, 128], BF16)

            def rotary(out, src, c, s, rk, wk, rtb, rtk):
                x1 = src[:, 0:64]; x2 = src[:, 64:128]
                tt(rtb[:, 0, :], x1, c, ALU.mult, rk, [rtk])
                tt(rtb[:, 1, :], x2, s, ALU.mult, rk, [rtk])
                tt(out[:, 0:64], rtb[:, 0, :], rtb[:, 1, :], ALU.subtract, [rtk], wk)
                tt(rtb[:, 2, :], x1, s, ALU.mult, rk, [rtk])
                tt(rtb[:, 3, :], x2, c, ALU.mult, rk, [rtk])
                tt(out[:, 64:128], rtb[:, 2, :], rtb[:, 3, :], ALU.add, [rtk], wk)

            eg = two("eg", [128, 128])
            NTT = int(os.environ.get('KRT', str(NT)))

            def load_wh(h):
                for ci, base in enumerate((1024, 2048, 3072, 4096)):
                    c0 = base + h * 128
                    dma('pool', wh[h % 2][:, :, ci * 128:(ci + 1) * 128],
                        w_in[:, c0:c0 + 128].rearrange("(k p) c -> p k c", p=128), (), ['wh%d' % (h % 2)])

            NHD = int(os.environ.get('KRH', '8'))
            load_wh(0)

            def ret_head(h):
                b = h % 2
                if h + 1 < NHD:
                    load_wh(h + 1)
                memset(R[:], 0.0, ['R'])
                memset(Rb[0][:], 0.0, ['Rb0'])

                def info(t):
                    p = t % 2
                    return p, str(p), (t >= 8), (t == 16), (1 if t == 16 else 0)

                def P0(t, half):
                    p, P, real, smp, mi = info(t)
                    ncol = 512 if real else 256
                    if half == 0:
                        dma('sp', hTc[p][:], hT_d[t], ['hT_d0', 'hT_d1'], ['hTc' + P])
                    w0 = 0 if real else 128
                    for k in range(half * 8, half * 8 + 8):
                        mm(ps[p][:, 0:ncol], hTc[p][:, k, :], wh[b][:, k, w0:w0 + ncol], k == 0, k == 15,
                           ['hTc' + P, 'wh%d' % b], [PK[p]])

                def D0(t):
                    p, P, real, smp, mi = info(t)
                    pp = ps[p]; ppk = PK[p]
                    if real:
                        n2, a0, vcol, src = 2, 0, 256, pp[:, 0:256]
                    else:
                        n2, a0, vcol, src = 1, 1, 128, pp[:, 0:128]
                    sv = src.rearrange("p (a h f) -> p a h f", a=n2, h=2)
                    ct_ = cqk_t[:, t, a0:a0 + n2, :]
                    st_ = sqk_t[:, t, a0:a0 + n2, :]
                    A = rA[p][:, 0:n2]; Bt = rB[p][:, 0:n2]
                    out = qk[p][:, a0:a0 + n2, :].rearrange("p a (h f) -> p a h f", h=2)
                    tt(A, sv, ct_.unsqueeze(2).broadcast_to([128, n2, 2, 64]), ALU.mult, [ppk, 'rot'], ['rA' + P])
                    tt(Bt[:, :, 0, :], sv[:, :, 1, :], st_, ALU.mult, [ppk, 'rot2'], ['rB' + P])
                    tt(Bt[:, :, 1, :], sv[:, :, 0, :], st_, ALU.mult, [ppk, 'rot2'], ['rB' + P])
                    tt(out[:, :, 0, :], A[:, :, 0, :], Bt[:, :, 0, :], ALU.subtract, ['rA' + P, 'rB' + P], ['qk' + P])
                    tt(out[:, :, 1, :], A[:, :, 1, :], Bt[:, :, 1, :], ALU.add, ['rA' + P, 'rB' + P], ['qk' + P])
                    act(vsb[p][:], pp[:, vcol:vcol + 128], AF.Copy, [ppk], ['vsb' + P])
                    ts(kd[p][:], qk[p][:, 1, :], kdec_t[:, mi, h:h + 1], None, ALU.mult, None,
                       ['qk' + P, 'kdec'], ['kd' + P])
                    if real:
                        act(eg[p][:], pp[:, 384:512], AF.Exp, [ppk], ['eg' + P], scale=-1.0)
                        ts(eg[p][:], eg[p][:], 1.0, None, ALU.add, None, ['eg' + P], ['eg' + P])
                        T.op('dve', lambda e, x=eg[p]: e.reciprocal(out=x[:], in_=x[:]), ['eg' + P], ['eg' + P])
                        tt(sgate[p][:], pp[:, 384:512], eg[p][:], ALU.mult, [ppk, 'eg' + P], ['sgate' + P])

                def S(t):
                    p, P, real, smp, mi = info(t)
                    if not smp:
                        p5 = ps[6 + p]; k5 = PK[6 + p]
                        mm(p5[:, 0:128], kd[p][:], vsb[p][:], True, True, ['kd' + P, 'vsb' + P], [k5])
                        stt(R[:], R[:], GAMMA[h] ** 128, p5[:, 0:128], ALU.mult, ALU.add, ['R', k5], ['R'])
                        cp(Rb[1 - p][:], R[:], ['R'], ['Rb%d' % (1 - p)], eng='pool')
                        if t == 15:
                            dma('sp', retp_o[h], R[:], ['R'], ['retp_o'])
                    else:
                        tt(km[:], kd[p][:].unsqueeze(1).broadcast_to([128, 16, 128]),
                           rowm_t[:].unsqueeze(2).broadcast_to([128, 16, 128]), ALU.mult, ['kd' + P, 'rowm'], ['km'])
                        for s_ in range(16):
                            bank = 4 + s_ // 4
                            mm(ps[bank][:, (s_ % 4) * 128:(s_ % 4 + 1) * 128], km[:, s_, :], vsb[p][:], True, True,
                               ['km', 'vsb' + P], [PK[bank]])
                        for b4 in range(4):
                            stt(r0f[:, b4 * 4:(b4 + 1) * 4, :], r0f[:, b4 * 4:(b4 + 1) * 4, :], GAMMA[h] ** 8,
                                ps[4 + b4][:, :].rearrange("p (s e) -> p s e", s=4), ALU.mult, ALU.add,
                                ['r0f', PK[4 + b4]], ['r0f'])
                        dma('sp', rets_o[:, h].rearrange("s d e -> d s e"), r0f[:], ['r0f'], ['rets_o'])

                def P1D1(t):
                    p, P, real, smp, mi = info(t)
                    p2 = ps[2 + p][:].bitcast(BF16); k2 = PK[2 + p]
                    tp(p2[:, 0:128], qk[p][:, 0, :], ident[:], ['qk' + P, 'ident'], [k2])
                    tp(p2[:, 128:256], qk[p][:, 1, :], ident[:], ['qk' + P, 'ident'], [k2])
                    cp(qkT[p][:], p2[:, 0:256], [k2], ['qkT' + P], eng='act')
                    tt(qdT[p][:], p2[:, 0:128], qdec_t[:, mi, h, :], ALU.mult, [k2, 'qdec'], ['qdT' + P])

                def P2D2(t):
                    p, P, real, smp, mi = info(t)
                    p3 = ps[4 + p]; k3 = PK[4 + p]
                    mm(p3[:, 0:128], qkT[p][:, 128:256], qkT[p][:, 0:128], True, True, ['qkT' + P], [k3])
                    tt(sT[p][:], p3[:, 0:128], maskT_t[:, mi, h, :], ALU.mult, [k3, 'maskT'], ['sT' + P])
                    if smp:
                        tt(qm[:], qdT[p][:].unsqueeze(1).broadcast_to([128, 16, 128]), bm01_t[:], ALU.mult,
                           ['qdT' + P, 'bm01'], ['qm'])

                def P3(t):
                    p, P, real, smp, mi = info(t)
                    p3 = ps[4 + p]; k3 = PK[4 + p]
                    mm(p3[:, 128:256], sT[p][:], vsb[p][:], True, False, ['sT' + P, 'vsb' + P], [k3])
                    if smp:
                        for s_ in range(16):
                            mm(p3[:, 128:256], qm[:, s_, :], r0b[:, s_, :], False, s_ == 15, ['qm', 'r0b'], [k3])
                    else:
                        mm(p3[:, 128:256], qdT[p][:], Rb[p][:], False, True, ['qdT' + P, 'Rb' + P], [k3])

                def D3(t):
                    p, P, real, smp, mi = info(t)
                    p3 = ps[4 + p]; k3 = PK[4 + p]
                    o_ps = p3[:, 128:256]
                    stp = st[p]
                    act(osb[p][:], o_ps, AF.Copy, [k3], ['osb' + P, 'sta' + P], accum=stp[:, 0:1])
                    act(osq[p][:], o_ps, AF.Square, [k3], ['osq' + P, 'stb' + P], accum=stp[:, 1:2])
                    ts(stp[:, 2:3], stp[:, 0:1], 1.0 / 128, None, ALU.mult, None, ['sta' + P], ['stc' + P])
                    stt(stp[:, 3:4], stp[:, 2:3], -1.0, stp[:, 2:3], ALU.mult, ALU.mult, ['stc' + P], ['std' + P])
                    stt(stp[:, 4:5], stp[:, 1:2], 1.0 / 128, stp[:, 3:4], ALU.mult, ALU.add,
                        ['stb' + P, 'std' + P], ['ste' + P])
                    act(stp[:, 5:6], stp[:, 4:5], AF.Ln, ['ste' + P, 'eps_t'], ['stf' + P], bias=eps_t[:, 0:1])
                    act(stp[:, 5:6], stp[:, 5:6], AF.Exp, ['stf' + P], ['stf' + P], scale=-0.5)
                    ts(osb[p][:], osb[p][:], stp[:, 2:3], stp[:, 5:6], ALU.subtract, ALU.mult,
                       ['osb' + P, 'stc' + P, 'stf' + P], ['osb' + P])
                    tt(osb[p][:], osb[p][:], gnw_t[:, h * 128:(h + 1) * 128], ALU.mult, ['osb' + P, 'gnw'], ['osb' + P])
                    tt(ro[p][:], osb[p][:], sgate[p][:], ALU.mult, ['osb' + P, 'sgate' + P], ['ro' + P])

                def P4D4(t):
                    p, P, real, smp, mi = info(t)
                    p2 = ps[2 + p][:].bitcast(BF16); k2 = PK[2 + p]
                    tp(p2[:, 256:384], ro[p][:], ident[:], ['ro' + P, 'ident'], [k2])
                    ri = t - 8
                    cp(mixT[:, 8 + h, ri * 128:(ri + 1) * 128], p2[:, 256:384], [k2], ['mixR'], eng='act')

                def valid(t):
                    return 0 <= t < NTT

                def realv(t):
                    return valid(t) and t >= 8

                for i in range(NTT + 2):
                    if i == min(8, NTT - 1):
                        dma('sp', r0f[:], r0[:, h].rearrange("s d e -> d s e"), (), ['r0f'])
                        cp(r0b[:], r0f[:], ['r0f'], ['r0b'], eng='pool')
                    if realv(i - 2):
                        D3(i - 2)
                    if valid(i - 1):
                        S(i - 1)
                    if realv(i - 1):
                        P1D1(i - 1)
                    if valid(i):
                        P0(i, 0)
                    if realv(i - 1):
                        P2D2(i - 1)
                    if valid(i):
                        P0(i, 1)
                        D0(i)
                    if realv(i - 1):
                        P3(i - 1)
                    if realv(i - 2):
                        P4D4(i - 2)

            for h in range(NHD):
                ret_head(h)

            if STOP_AT <= 6:
                return
            areset()
            xn = sb("xn", [128, NR, D])
            h2T = sb("h2T", [128, NR, 16, 128], BF16)
            gam = sb("gam", [128, D])
            mark3 = ar['off']
            hb = [sb("hb%d" % i, [128, D], BF16) for i in range(2)]
            for ri in range(NR):
                dma('sp', xn[:, ri, :], xs[8 + ri], (), ['xn%d' % ri])
            wo = [sb("wo%d" % i, [128, 16, 512], BF16) for i in range(2)]
            for cb in range(4):
                b = cb % 2
                dma('pool', wo[b][:], w_out[:, cb * 512:(cb + 1) * 512].rearrange("(k p) c -> p k c", p=128),
                    (), ['wo%d' % b])
                for ri in range(NR):
                    pb = ri % 2
                    for k in range(16):
                        mm(ps[pb][:], mixT[:, k, ri * 128:(ri + 1) * 128], wo[b][:, k, :], k == 0, k == 15,
                           ['mixS', 'mixR', 'wo%d' % b], [PK[pb]])
                    tt(xn[:, ri, cb * 512:(cb + 1) * 512], xn[:, ri, cb * 512:(cb + 1) * 512], ps[pb][:], ALU.add,
                       ['xn%d' % ri, PK[pb]], ['xn%d' % ri])

            if STOP_AT <= 7:
                return
            dma('sp', gam[:], g3[1], (), ['gam'])
            norm_s1(xn[:, 0, :], 'xn0', gam[:], hb[0][:], 'hb0', 0)
            for ri in range(NR):
                b = ri % 2
                if ri + 1 < NR:
                    norm_s1(xn[:, ri + 1, :], 'xn%d' % (ri + 1), gam[:], hb[1 - b][:], 'hb%d' % (1 - b), 1 - b)
                norm_s2(h2T[:, ri], 'hT%d' % ri, hb[b][:], 'hb%d' % b)

            if STOP_AT <= 8:
                return
            G = 4
            T.barrier()
            ar['off'] = mark3
            wgu = [sb("wgu%d" % i, [128, 16, 2, 128], BF16) for i in range(2)]
            w2 = [sb("w2_0", [128, G, D], BF16)] * 2
            actT = [sb("actT0", [128, G, NTOK], BF16)] * 2
            sil = [sb("sil%d" % i, [128, 512]) for i in range(2)]
            h2keys = ['hT%d' % i for i in range(NR)]
            n = 0
            for gi in range(DFF // 128 // G):
                gb = gi % 2
                for fi in range(G):
                    fc = gi * G + fi
                    b = fc % 2
                    dma('pool', wgu[b][:, :, 0, :], w_f1[:, fc * 128:(fc + 1) * 128].rearrange("(k p) c -> p k c", p=128),
                        (), ['wgu%d' % b])
                    dma('pool', wgu[b][:, :, 1, :],
                        w_f1[:, DFF + fc * 128:DFF + (fc + 1) * 128].rearrange("(k p) c -> p k c", p=128),
                        (), ['wgu%d' % b])
                    if fi == G - 1:
                        dma('pool', w2[gb][:],
                            w_f2[gi * G * 128:(gi + 1) * G * 128, :].rearrange("(k p) c -> p k c", p=128), (), ['w2_0'])
                    for (c0, cn) in blocks:
                        pb = (n % 2) * 2
                        t0 = c0 // 128
                        ntl = cn // 128
                        for k in range(16):
                            mm(ps[pb][:, 0:cn], wgu[b][:, k, 0, :], h2T[:, t0:t0 + ntl, k, :], k == 0, k == 15,
                               ['wgu%d' % b] + h2keys[t0:t0 + ntl], [PK[pb]])
                        for k in range(16):
                            mm(ps[pb + 1][:, 0:cn], wgu[b][:, k, 1, :], h2T[:, t0:t0 + ntl, k, :], k == 0, k == 15,
                               ['wgu%d' % b] + h2keys[t0:t0 + ntl], [PK[pb + 1]])
                        sb_ = n % 2
                        act(sil[sb_][:, 0:cn], ps[pb][:, 0:cn], AF.Silu, [PK[pb]], ['sil%d' % sb_])
                        tt(actT[gb][:, fi, c0:c0 + cn], sil[sb_][:, 0:cn], ps[pb + 1][:, 0:cn], ALU.mult,
                           ['sil%d' % sb_, PK[pb + 1]], ['actT0'])
                        n += 1
                for ri in range(NR):
                    for cb in range(4):
                        bank = 4 + cb
                        for k in range(G):
                            mm(ps[bank][:], actT[gb][:, k, ri * 128:(ri + 1) * 128], w2[gb][:, k, cb * 512:(cb + 1) * 512],
                               k == 0, k == G - 1, ['actT0', 'w2_0'], [PK[bank]])
                        tt(xn[:, ri, cb * 512:(cb + 1) * 512], xn[:, ri, cb * 512:(cb + 1) * 512], ps[bank][:], ALU.add,
                           ['xn%d' % ri, PK[bank]], ['xn%d' % ri], eng='dve')

            if STOP_AT <= 9:
                return
            dma('sp', gam[:], g3[2], (), ['gam'])
            T.barrier()
            ar['off'] = mark3
            yo = [sb("yo%d" % i, [128, D]) for i in range(2)]
            sqs = sb("sqs", [128, D])
            for ri in range(NR):
                b = ri % 2
                act(sqs[:], xn[:, ri, :], AF.Square, ['xn%d' % ri], ['sqs', 'small8'], accum=small[:, 8:9])
                ts(small[:, 9:10], small[:, 8:9], 1.0 / D, EPS, ALU.mult, ALU.add, ['small8'], ['small9'])
                act(small[:, 9:10], small[:, 9:10], AF.Sqrt, ['small9'], ['small9'])
                T.op('dve', lambda e: e.reciprocal(out=small[:, 9:10], in_=small[:, 9:10]), ['small9'], ['small9'])
                stt(yo[b][:], xn[:, ri, :], small[:, 9:10], gam[:], ALU.mult, ALU.mult,
                    ['xn%d' % ri, 'small9', 'gam'], ['yo%d' % b])
                dma('sp', y_o[ri], yo[b][:], ['yo%d' % b], ['y_o%d' % ri])

        _phases()
        T.finish()
        with nc.Block() as block:
            @block.tensor
            def _(e):
                T.replay('pe', e)

            @block.vector
            def _(e):
                T.replay('dve', e)

            @block.scalar
            def _(e):
                T.replay('act', e)

            @block.gpsimd
            def _(e):
                T.replay('pool', e)

            @block.sync
            def _(e):
                T.replay('sp', e)
    return nc


def _host_consts(core):
    half = core % 2
    inv_freq = (10000.0 ** (-np.arange(64, dtype=np.float32) / np.float32(64))).astype(np.float32)
    pos = np.zeros((NT, 128), np.float32)
    for t in range(8):
        pos[t] = t * 128 + np.arange(128)
        pos[8 + t] = half * 1024 + t * 128 + np.arange(128)
    pos[16] = 16384 + (np.arange(128) % 8)
    ang = (pos[:, :, None].astype(np.float32) * inv_freq[None, None, :]).astype(np.float32)
    c = np.cos(ang.astype(np.float64)).astype(np.float32)
    s = np.sin(ang.astype(np.float64)).astype(np.float32)
    sc = np.float32(128 ** -0.5)
    lg = np.log(np.array(GAMMA, np.float64))
    idx = np.arange(128)
    maskT = np.zeros((2, 128, 8, 128), np.float32)
    qdec = np.zeros((2, 128, 8, 128), np.float32)
    kdec = np.zeros((2, 128, 8), np.float32)
    for h in range(8):
        dm = idx[None, :] - idx[:, None]
        maskT[0, :, h, :] = np.where(dm >= 0, np.exp(lg[h] * np.maximum(dm, 0)), 0.0)
        same = (idx[None, :] // 8) == (idx[:, None] // 8)
        maskT[1, :, h, :] = np.where((dm >= 0) & same, np.exp(lg[h] * np.maximum(dm, 0)), 0.0)
        qdec[0, :, h, :] = np.exp(lg[h] * (idx + 1.0))[None, :]
        qdec[1, :, h, :] = np.exp(lg[h] * ((idx % 8) + 1.0))[None, :]
        kdec[0, :, h] = np.exp(lg[h] * (127.0 - idx))
        kdec[1, :, h] = np.exp(lg[h] * (7.0 - (idx % 8)))
    bm01 = np.zeros((128, 16, 128), np.float32)
    rowm = np.zeros((128, 16), np.float32)
    for s_ in range(16):
        bm01[:, s_, s_ * 8:(s_ + 1) * 8] = 1.0
        rowm[s_ * 8:(s_ + 1) * 8, s_] = 1.0
    cqk = np.stack([c, (c * sc).astype(np.float32)], axis=2)
    sqk = np.stack([s, (s * sc).astype(np.float32)], axis=2)
    return dict(cqk=np.ascontiguousarray(cqk), sqk=np.ascontiguousarray(sqk),
                maskT=maskT, qdec=qdec, kdec=kdec, bm01=bm01, rowm=rowm,
                idn=np.eye(128, dtype=np.float32))


def _sm(a):
    return np.ascontiguousarray(a.reshape(32, 2, 64).transpose(1, 2, 0).reshape(128, 32))


_NC = None


def _prepare(x_prompt, x_sample, state_s5_re, state_s5_im, state_ret,
           norm_mix, w_in, s5_lambda_re, s5_lambda_im, s5_log_step, s5_b_re, s5_b_im,
           s5_c_re, s5_c_im, s5_d, s5_w_glu, s5_b_glu, ret_gn_w, w_out, norm_ffn,
           w_ffn_in, w_ffn_out, norm_final):
    global _NC
    f = lambda a: np.ascontiguousarray(np.asarray(a, dtype=np.float32))
    x_prompt, x_sample = f(x_prompt), f(x_sample)
    g3 = np.stack([np.broadcast_to(f(v).reshape(1, D), (128, D)) for v in (norm_mix[0], norm_ffn[0], norm_final)])
    gnw = np.ascontiguousarray(np.broadcast_to(f(ret_gn_w[0]).reshape(1, 1024), (128, 1024)))
    lam = np.stack([_sm(f(s5_lambda_re[0])), _sm(f(s5_lambda_im[0])),
                    _sm(np.broadcast_to(f(s5_log_step[0]).reshape(64, 1), (64, 64)))], axis=1)
    bre, bim = f(s5_b_re[0]), f(s5_b_im[0])
    cre, cim = f(s5_c_re[0]), f(s5_c_im[0])
    btp = np.zeros((128, 32, 2, 128), np.float32)
    ctp = np.zeros((128, 32, 2, 128), np.float32)
    for j in range(32):
        q = j % 4
        for hf in range(2):
            g = 2 * j + hf
            rows = slice(32 * q + 16 * hf, 32 * q + 16 * hf + 16)
            cols = slice(64 * hf, 64 * hf + 64)
            btp[rows, j, 0, cols] = bre[g].T
            btp[rows, j, 1, cols] = bim[g].T
            ctp[cols, j, 0, rows] = cre[g].T
            ctp[cols, j, 1, rows] = cim[g].T
    dsm = np.ascontiguousarray(f(s5_d[0]).reshape(8, 128).T)
    bglu = np.ascontiguousarray(f(s5_b_glu[0]).reshape(8, 128).T)
    shared = dict(w_in=f(w_in[0]), w_out=f(w_out[0]), w_f1=f(w_ffn_in[0]), w_f2=f(w_ffn_out[0]),
                  w_glu=f(s5_w_glu[0]), g3=np.ascontiguousarray(g3), gnw=gnw, lam=np.ascontiguousarray(lam),
                  btp=btp, ctp=ctp, dsm=dsm, bglu=bglu)
    sre, sim, sret = f(state_s5_re[0]), f(state_s5_im[0]), f(state_ret[0])
    in_maps = []
    for c in range(8):
        seq, half = c // 2, c % 2
        xs = np.zeros((NT, 128, D), np.float32)
        if half == 1:
            xs[0:8] = x_prompt[seq, 0:1024].reshape(8, 128, D)
        xs[8:16] = x_prompt[seq, half * 1024:(half + 1) * 1024].reshape(8, 128, D)
        xs[16] = x_sample[c * 16:(c + 1) * 16].reshape(128, D)
        s5i = np.zeros((128, 16, 2, 32), np.float32)
        for s_ in range(16):
            s5i[:, s_, 0, :] = _sm(sre[c * 16 + s_])
            s5i[:, s_, 1, :] = _sm(sim[c * 16 + s_])
        m = dict(shared)
        m.update(_host_consts(c))
        m.update(xs=xs, s5i=s5i, r0=np.ascontiguousarray(sret[c * 16:(c + 1) * 16]))
        in_maps.append(m)
    return in_maps


def kernel(**inputs):
    global _NC
    in_maps = _prepare(**inputs)
    if _NC is None:
        _NC = build()
    res = run_bass_kernel_spmd(_NC, in_maps, core_ids=list(range(8)))
    return _assemble(res.results)


def _assemble(rs):
    y_prompt = np.zeros((4, 2048, D), np.float32)
    y_sample = np.zeros((128, 8, D), np.float32)
    p_re = np.zeros((1, 4, 64, 64), np.float32); p_im = np.zeros((1, 4, 64, 64), np.float32)
    p_ret = np.zeros((1, 4, 8, 128, 128), np.float32)
    s_re = np.zeros((1, 128, 64, 64), np.float32); s_im = np.zeros((1, 128, 64, 64), np.float32)
    s_ret = np.zeros((1, 128, 8, 128, 128), np.float32)

    def unsm(a):
        return a.reshape(2, 64, 32).transpose(2, 0, 1).reshape(64, 64)

    for c in range(8):
        seq, half = c // 2, c % 2
        r = rs[c]
        y = np.asarray(r["y"], np.float32)
        y_prompt[seq, half * 1024:(half + 1) * 1024] = y[0:8].reshape(1024, D)
        y_sample[c * 16:(c + 1) * 16] = y[8].reshape(16, 8, D)
        if half == 1:
            sp = np.asarray(r["s5p"], np.float32)
            p_re[0, seq] = unsm(sp[:, 0, :]); p_im[0, seq] = unsm(sp[:, 1, :])
            p_ret[0, seq] = np.asarray(r["retp"], np.float32)
        ss = np.asarray(r["s5s"], np.float32)
        for s_ in range(16):
            s_re[0, c * 16 + s_] = unsm(ss[:, s_, 0, :]); s_im[0, c * 16 + s_] = unsm(ss[:, s_, 1, :])
        s_ret[0, c * 16:(c + 1) * 16] = np.asarray(r["rets"], np.float32)
    return (y_prompt, y_sample, p_re, p_im, p_ret, s_re, s_im, s_ret)
```
